# Optimizing a Trainium2 kernel written in Bass

```python
import jax, jax.numpy as jnp
from jax import lax
import numpy as np

D_MODEL = 1024
BATCH = 2
SEQ = 8192
DEPTH = 2
DEC_BATCH = 32
DEC_SEQ = 4
PAST_LEN = 8192
PAGE_SIZE = 128

N_HEADS = 16
HEAD_DIM = D_MODEL // N_HEADS
MIX_WIDTH = N_HEADS * HEAD_DIM
ROT_DIM = HEAD_DIM // 4
ROPE_THETA = 500000.0
NORM_EPS = 1e-6
MOBA_BLOCK = 256
MOBA_TOPK = 3
NSA_KV_GROUPS = 4
NSA_REP = N_HEADS // NSA_KV_GROUPS
KV_WIDTH = NSA_KV_GROUPS * HEAD_DIM
CMP_LEN = 32
CMP_STRIDE = 16
CMP_HIDDEN = 2 * HEAD_DIM
SEL_BLOCK = 64
SEL_TOPN = 16
WINDOW = 512
Q_CHUNK = 32
N_MOBA_LAYERS = (DEPTH + 1) // 2
N_NSA_LAYERS = DEPTH // 2
NSA_IN_WIDTH = MIX_WIDTH + 6 * KV_WIDTH + 3 * N_HEADS + MIX_WIDTH

kernel_name = 'hybrid_moba_nsa_decode_step'


def rms_norm(x, g):
    xf = x.astype(jnp.float32)
    y = xf * lax.rsqrt(jnp.mean(xf * xf, axis=-1, keepdims=True) + NORM_EPS)
    return (y * g.astype(jnp.float32)).astype(x.dtype)


def partial_rope(x, pos):
    half = ROT_DIM // 2
    inv_freq = jnp.float32(ROPE_THETA) ** (-jnp.arange(half, dtype=jnp.float32) / half)
    ang = pos.astype(jnp.float32)[:, None] * inv_freq[None, :]
    cos = jnp.cos(ang)[:, None, :]
    sin = jnp.sin(ang)[:, None, :]
    xr = x[..., :ROT_DIM].astype(jnp.float32)
    x1, x2 = xr[..., :half], xr[..., half:]
    rot = jnp.concatenate([x1 * cos - x2 * sin, x2 * cos + x1 * sin], axis=-1)
    return jnp.concatenate([rot.astype(x.dtype), x[..., ROT_DIM:]], axis=-1)


def masked_softmax(s, mask):
    s = jnp.where(mask, s.astype(jnp.float32), -jnp.inf)
    m = jnp.max(s, axis=-1, keepdims=True)
    m = jnp.where(jnp.isfinite(m), m, 0.0)
    p = jnp.where(mask, jnp.exp(s - m), 0.0)
    return p / jnp.maximum(jnp.sum(p, axis=-1, keepdims=True), 1e-30)


def to_blocks(k, blk):
    t = k.shape[2]
    nb = -(-t // blk)
    k = jnp.pad(k, ((0, 0), (0, 0), (0, nb * blk - t), (0, 0)))
    return k.reshape(k.shape[0], k.shape[1], nb, blk, k.shape[-1])


def map_query_chunks(fn, q, pos):
    s = q.shape[-2]
    nc = s // Q_CHUNK
    qc = jnp.moveaxis(q.reshape(q.shape[:-2] + (nc, Q_CHUNK, q.shape[-1])), -3, 0)
    pc = pos.reshape(nc, Q_CHUNK)
    out = lax.map(lambda a: fn(a[0], a[1]), (qc, pc))
    out = jnp.moveaxis(out, 0, -3)
    return out.reshape(out.shape[:-3] + (s, out.shape[-1]))


def gather_pages(pool, layer, page_table):
    g = pool[layer, page_table]
    return g.reshape((g.shape[0], g.shape[1] * g.shape[2]) + g.shape[3:])


def gated_out(o, z, w_out):
    return (o * jax.nn.silu(z)) @ w_out


def moba_core(q, q_pos, kb, vb, kmean):
    b, h, nq, d = q.shape
    nb = kb.shape[2]
    cur = q_pos // MOBA_BLOCK
    s_blk = jnp.einsum('bhqd,bhnd->bhqn', q, kmean).astype(jnp.float32)
    past = jnp.arange(nb)[None, :] < cur[:, None]
    s_blk = jnp.where(past, s_blk, -jnp.inf)
    kk = min(MOBA_TOPK, nb)
    vals, idx = lax.top_k(s_blk, kk)
    valid = jnp.isfinite(vals)
    idx_all = jnp.concatenate([idx, jnp.broadcast_to(cur[:, None], idx.shape[:-1] + (1,)).astype(idx.dtype)], axis=-1)
    valid_all = jnp.concatenate([valid, jnp.ones(valid.shape[:-1] + (1,), bool)], axis=-1)
    bi = jnp.arange(b)[:, None, None, None]
    hi = jnp.arange(h)[None, :, None, None]
    nj = kk + 1
    kg = kb[bi, hi, idx_all].reshape(b, h, nq, nj * MOBA_BLOCK, d)
    vg = vb[bi, hi, idx_all].reshape(b, h, nq, nj * MOBA_BLOCK, d)
    kpos = (idx_all[..., None] * MOBA_BLOCK + jnp.arange(MOBA_BLOCK)).reshape(b, h, nq, nj * MOBA_BLOCK)
    mask = jnp.repeat(valid_all, MOBA_BLOCK, axis=-1) & (kpos <= q_pos[:, None])
    s = jnp.einsum('bhqd,bhqkd->bhqk', q, kg) * (HEAD_DIM ** -0.5)
    p = masked_softmax(s, mask)
    return jnp.einsum('bhqk,bhqkd->bhqd', p.astype(vg.dtype), vg)


def moba_project(x, pos, norm_g, w_in, q_g, k_g):
    b, t, _ = x.shape
    q, k, v, z = jnp.split(rms_norm(x, norm_g) @ w_in, 4, axis=-1)
    heads = lambda a: a.reshape(b, t, N_HEADS, HEAD_DIM)
    q = partial_rope(rms_norm(heads(q), q_g), pos)
    k = partial_rope(rms_norm(heads(k), k_g), pos)
    return q, k, heads(v), z


def moba_layer(x_p, x_s, kv_past, norm_g, w_in, q_g, k_g, w_out):
    b, s, _ = x_p.shape
    pos_p = jnp.arange(s, dtype=jnp.int32)
    q, k, v, z = moba_project(x_p, pos_p, norm_g, w_in, q_g, k_g)
    kb = to_blocks(k.transpose(0, 2, 1, 3), MOBA_BLOCK)
    vb = to_blocks(v.transpose(0, 2, 1, 3), MOBA_BLOCK)
    kmean = kb.mean(axis=3)
    o = map_query_chunks(lambda qc, pc: moba_core(qc, pc, kb, vb, kmean), q.transpose(0, 2, 1, 3), pos_p)
    y_p = x_p + gated_out(o.transpose(0, 2, 1, 3).reshape(b, s, MIX_WIDTH), z, w_out)
    db, ds, _ = x_s.shape
    p_len = kv_past.shape[1]
    pos_s = p_len + jnp.arange(ds, dtype=jnp.int32)
    qs, ks, vs, zs = moba_project(x_s, pos_s, norm_g, w_in, q_g, k_g)
    kb_s = to_blocks(jnp.concatenate([kv_past[:, :, 0], ks], axis=1).transpose(0, 2, 1, 3), MOBA_BLOCK)
    vb_s = to_blocks(jnp.concatenate([kv_past[:, :, 1], vs], axis=1).transpose(0, 2, 1, 3), MOBA_BLOCK)
    o_s = moba_core(qs.transpose(0, 2, 1, 3), pos_s, kb_s, vb_s, kb_s.mean(axis=3))
    y_s = x_s + gated_out(o_s.transpose(0, 2, 1, 3).reshape(db, ds, MIX_WIDTH), zs, w_out)
    return y_p, y_s, jnp.stack([k, v], axis=2), jnp.stack([ks, vs], axis=2)


def compress(x, pe, w1, b1, w2):
    t = x.shape[2]
    nc = (t - CMP_LEN) // CMP_STRIDE + 1
    idx = jnp.arange(nc)[:, None] * CMP_STRIDE + jnp.arange(CMP_LEN)[None, :]
    blocks = x[:, :, idx] + pe
    flat = blocks.reshape(blocks.shape[:3] + (CMP_LEN * HEAD_DIM,))
    return jax.nn.gelu(flat @ w1 + b1) @ w2


def cmp_to_sel(nc, ns):
    st = jnp.arange(nc)[:, None] * CMP_STRIDE
    j0 = jnp.arange(ns)[None, :] * SEL_BLOCK
    return ((st < j0 + SEL_BLOCK) & (st + CMP_LEN > j0)).astype(jnp.float32)


def nsa_core(q, q_pos, kc, vc, overlap, ksb, vsb, kw, vw, kw_pos):
    b, g, r, nq, d = q.shape
    scale = HEAD_DIM ** -0.5
    nc = kc.shape[2]
    cmp_end = jnp.arange(nc) * CMP_STRIDE + CMP_LEN - 1
    cmask = cmp_end[None, :] <= q_pos[:, None]
    p_c = masked_softmax(jnp.einsum('bgrqd,bgnd->bgrqn', q, kc) * scale, cmask)
    o_c = jnp.einsum('bgrqn,bgnd->bgrqd', p_c.astype(vc.dtype), vc)
    ns = ksb.shape[2]
    imp = p_c.sum(axis=2) @ overlap
    cur = q_pos // SEL_BLOCK
    j = jnp.arange(ns)[None, :]
    forced = (j == 0) | (j == cur[:, None]) | (j == cur[:, None] - 1)
    allowed = j <= cur[:, None]
    imp = jnp.where(forced, jnp.inf, jnp.where(allowed, imp, -jnp.inf))
    kk = min(SEL_TOPN, ns)
    vals, idx = lax.top_k(imp, kk)
    valid = vals > -jnp.inf
    bi = jnp.arange(b)[:, None, None, None]
    gi = jnp.arange(g)[None, :, None, None]
    ks = ksb[bi, gi, idx].reshape(b, g, nq, kk * SEL_BLOCK, d)
    vs = vsb[bi, gi, idx].reshape(b, g, nq, kk * SEL_BLOCK, d)
    kpos = (idx[..., None] * SEL_BLOCK + jnp.arange(SEL_BLOCK)).reshape(b, g, nq, kk * SEL_BLOCK)
    smask = jnp.repeat(valid, SEL_BLOCK, axis=-1) & (kpos <= q_pos[:, None])
    p_s = masked_softmax(jnp.einsum('bgrqd,bgqkd->bgrqk', q, ks) * scale, smask[:, :, None])
    o_s = jnp.einsum('bgrqk,bgqkd->bgrqd', p_s.astype(vs.dtype), vs)
    wmask = (kw_pos[None, :] <= q_pos[:, None]) & (kw_pos[None, :] > q_pos[:, None] - WINDOW) & (kw_pos[None, :] >= 0)
    p_w = masked_softmax(jnp.einsum('bgrqd,bgkd->bgrqk', q, kw) * scale, wmask)
    o_w = jnp.einsum('bgrqk,bgkd->bgrqd', p_w.astype(vw.dtype), vw)
    return jnp.stack([o_c, o_s, o_w], axis=0)


def nsa_project(x, pos, norm_g, w_in, q_g, k_g):
    b, t, _ = x.shape
    splits = np.cumsum([MIX_WIDTH] + [KV_WIDTH] * 6 + [3 * N_HEADS]).tolist()
    q, kc, vc, ks, vs, kw, vw, gt, z = jnp.split(rms_norm(x, norm_g) @ w_in, splits, axis=-1)
    kvh = lambda a: a.reshape(b, t, NSA_KV_GROUPS, HEAD_DIM)
    q = partial_rope(rms_norm(q.reshape(b, t, N_HEADS, HEAD_DIM), q_g), pos)
    kc = partial_rope(kvh(kc), pos)
    ks = partial_rope(rms_norm(kvh(ks), k_g[1]), pos)
    kw = partial_rope(rms_norm(kvh(kw), k_g[2]), pos)
    gates = jax.nn.sigmoid(gt).reshape(b, t, NSA_KV_GROUPS, NSA_REP, 3)
    return q, kc, kvh(vc), ks, kvh(vs), kw, kvh(vw), gates, z


def nsa_keys(kc, vc, ks, vs, k_g, cmp_pe, cmp_w1, cmp_b1, cmp_w2):
    tr = lambda a: a.transpose(0, 2, 1, 3)
    kcmp = rms_norm(compress(tr(kc), cmp_pe[0], cmp_w1[0], cmp_b1[0], cmp_w2[0]), k_g[0])
    vcmp = compress(tr(vc), cmp_pe[1], cmp_w1[1], cmp_b1[1], cmp_w2[1])
    ksb = to_blocks(tr(ks), SEL_BLOCK)
    vsb = to_blocks(tr(vs), SEL_BLOCK)
    return kcmp, vcmp, cmp_to_sel(kcmp.shape[2], ksb.shape[2]), ksb, vsb


def group_q(q):
    b, t = q.shape[:2]
    return q.reshape(b, t, NSA_KV_GROUPS, NSA_REP, HEAD_DIM).transpose(0, 2, 3, 1, 4)


def combine(o, gates):
    b, t = gates.shape[:2]
    return jnp.einsum('cbgrtd,btgrc->btgrd', o, gates).reshape(b, t, MIX_WIDTH)


def nsa_layer(x_p, x_s, kv_past, win_past, norm_g, w_in, q_g, k_g, cmp_pe, cmp_w1, cmp_b1, cmp_w2, w_out):
    b, s, _ = x_p.shape
    pos_p = jnp.arange(s, dtype=jnp.int32)
    q, kc, vc, ks, vs, kw, vw, gates, z = nsa_project(x_p, pos_p, norm_g, w_in, q_g, k_g)
    kcmp, vcmp, ov, ksb, vsb = nsa_keys(kc, vc, ks, vs, k_g, cmp_pe, cmp_w1, cmp_b1, cmp_w2)
    pad_w = lambda a: jnp.pad(a.transpose(0, 2, 1, 3), ((0, 0), (0, 0), (WINDOW, 0), (0, 0)))
    kw_pad, vw_pad = pad_w(kw), pad_w(vw)

    def chunk(qc, pc):
        p0 = pc[0]
        kwc = lax.dynamic_slice_in_dim(kw_pad, p0, Q_CHUNK + WINDOW, axis=2)
        vwc = lax.dynamic_slice_in_dim(vw_pad, p0, Q_CHUNK + WINDOW, axis=2)
        kw_pos = p0 - WINDOW + jnp.arange(Q_CHUNK + WINDOW, dtype=jnp.int32)
        return nsa_core(qc, pc, kcmp, vcmp, ov, ksb, vsb, kwc, vwc, kw_pos)

    o = map_query_chunks(chunk, group_q(q), pos_p)
    y_p = x_p + gated_out(combine(o, gates), z, w_out)
    win_len_p = min(WINDOW, s)
    rows_p = jnp.stack([kc, vc, ks, vs], axis=2)
    win_p = jnp.stack([kw, vw], axis=2)[:, s - win_len_p:]
    db, ds, _ = x_s.shape
    p_len = kv_past.shape[1]
    wb = win_past.shape[1]
    pos_s = p_len + jnp.arange(ds, dtype=jnp.int32)
    qs, kc_s, vc_s, ks_s, vs_s, kw_s, vw_s, gates_s, zs = nsa_project(x_s, pos_s, norm_g, w_in, q_g, k_g)
    cat = lambda past, new: jnp.concatenate([past, new], axis=1)
    kcmp_s, vcmp_s, ov_s, ksb_s, vsb_s = nsa_keys(cat(kv_past[:, :, 0], kc_s), cat(kv_past[:, :, 1], vc_s),
                                                  cat(kv_past[:, :, 2], ks_s), cat(kv_past[:, :, 3], vs_s),
                                                  k_g, cmp_pe, cmp_w1, cmp_b1, cmp_w2)
    win_all = cat(win_past, jnp.stack([kw_s, vw_s], axis=2))
    kw_pos_s = p_len - wb + jnp.arange(wb + ds, dtype=jnp.int32)
    o_s = nsa_core(group_q(qs), pos_s, kcmp_s, vcmp_s, ov_s, ksb_s, vsb_s,
                   win_all[:, :, 0].transpose(0, 2, 1, 3), win_all[:, :, 1].transpose(0, 2, 1, 3), kw_pos_s)
    y_s = x_s + gated_out(combine(o_s, gates_s), zs, w_out)
    rows_s = jnp.stack([kc_s, vc_s, ks_s, vs_s], axis=2)
    return y_p, y_s, rows_p, rows_s, win_p, win_all[:, ds:]


def setup_inputs(seed: int = 0) -> dict:
    key = jax.random.key(seed)
    ks = jax.random.split(key, 24)
    nrm = lambda k, shape, sc: sc * jax.random.normal(k, shape, jnp.float32)
    n_pages = PAST_LEN // PAGE_SIZE
    n_pool = (DEC_BATCH * n_pages * 5) // 4
    win_len = min(WINDOW, PAST_LEN)
    page_table = jax.random.permutation(ks[5], n_pool)[: DEC_BATCH * n_pages].reshape(DEC_BATCH, n_pages).astype(jnp.int32)
    return {
        'x_prompt': nrm(ks[0], (BATCH, SEQ, D_MODEL), 1.0),
        'x_sample': nrm(ks[1], (DEC_BATCH, DEC_SEQ, D_MODEL), 1.0),
        'cache_moba_kv': nrm(ks[2], (N_MOBA_LAYERS, n_pool, PAGE_SIZE, 2, N_HEADS, HEAD_DIM), 1.0),
        'cache_nsa_kv': nrm(ks[3], (N_NSA_LAYERS, n_pool, PAGE_SIZE, 4, NSA_KV_GROUPS, HEAD_DIM), 1.0),
        'state_nsa_win': nrm(ks[4], (N_NSA_LAYERS, DEC_BATCH, win_len, 2, NSA_KV_GROUPS, HEAD_DIM), 1.0),
        'page_table': page_table,
        'a_norm': 1.0 + nrm(ks[6], (N_MOBA_LAYERS, D_MODEL), 0.02),
        'a_w_in': nrm(ks[7], (N_MOBA_LAYERS, D_MODEL, 4 * MIX_WIDTH), D_MODEL ** -0.5),
        'a_q_norm': 1.0 + nrm(ks[8], (N_MOBA_LAYERS, HEAD_DIM), 0.02),
        'a_k_norm': 1.0 + nrm(ks[9], (N_MOBA_LAYERS, HEAD_DIM), 0.02),
        'a_w_out': nrm(ks[10], (N_MOBA_LAYERS, MIX_WIDTH, D_MODEL), MIX_WIDTH ** -0.5),
        'b_norm': 1.0 + nrm(ks[11], (N_NSA_LAYERS, D_MODEL), 0.02),
        'b_w_in': nrm(ks[12], (N_NSA_LAYERS, D_MODEL, NSA_IN_WIDTH), D_MODEL ** -0.5),
        'b_q_norm': 1.0 + nrm(ks[13], (N_NSA_LAYERS, HEAD_DIM), 0.02),
        'b_k_norm': 1.0 + nrm(ks[14], (N_NSA_LAYERS, 3, HEAD_DIM), 0.02),
        'b_cmp_pe': nrm(ks[15], (N_NSA_LAYERS, 2, CMP_LEN, HEAD_DIM), 0.1),
        'b_cmp_w1': nrm(ks[16], (N_NSA_LAYERS, 2, CMP_LEN * HEAD_DIM, CMP_HIDDEN), (CMP_LEN * HEAD_DIM) ** -0.5),
        'b_cmp_b1': nrm(ks[17], (N_NSA_LAYERS, 2, CMP_HIDDEN), 0.02),
        'b_cmp_w2': nrm(ks[18], (N_NSA_LAYERS, 2, CMP_HIDDEN, HEAD_DIM), CMP_HIDDEN ** -0.5),
        'b_w_out': nrm(ks[19], (N_NSA_LAYERS, MIX_WIDTH, D_MODEL), MIX_WIDTH ** -0.5),
    }


def reference(x_prompt, x_sample, cache_moba_kv, cache_nsa_kv, state_nsa_win, page_table,
              a_norm, a_w_in, a_q_norm, a_k_norm, a_w_out,
              b_norm, b_w_in, b_q_norm, b_k_norm, b_cmp_pe, b_cmp_w1, b_cmp_b1, b_cmp_w2, b_w_out):
    xp, xs = x_prompt, x_sample
    moba_p, moba_s, nsa_p, nsa_s, win_p, win_s = [], [], [], [], [], []
    for layer in range(DEPTH):
        i = layer // 2
        if layer % 2 == 0:
            xp, xs, kvp, kvs = moba_layer(xp, xs, gather_pages(cache_moba_kv, i, page_table),
                                          a_norm[i], a_w_in[i], a_q_norm[i], a_k_norm[i], a_w_out[i])
            moba_p.append(kvp)
            moba_s.append(kvs)
        else:
            xp, xs, rp, rs, wp, ws = nsa_layer(xp, xs, gather_pages(cache_nsa_kv, i, page_table), state_nsa_win[i],
                                               b_norm[i], b_w_in[i], b_q_norm[i], b_k_norm[i], b_cmp_pe[i],
                                               b_cmp_w1[i], b_cmp_b1[i], b_cmp_w2[i], b_w_out[i])
            nsa_p.append(rp)
            nsa_s.append(rs)
            win_p.append(wp)
            win_s.append(ws)
    return (xp, xs, jnp.stack(moba_p), jnp.stack(moba_s), jnp.stack(nsa_p), jnp.stack(nsa_s),
            jnp.stack(win_p), jnp.stack(win_s))
```

```python
import os
import numpy as np
from contextlib import ExitStack
import concourse.bass as bass
import concourse.mybir as mybir
from concourse.bass_utils import run_bass_kernel_spmd

F32 = mybir.dt.float32
BF16 = mybir.dt.bfloat16
I32 = mybir.dt.int32
AF = mybir.ActivationFunctionType
ALU = mybir.AluOpType
AX = mybir.AxisListType

D = 1024
NH = 16
HD = 64
EPS = 1e-6
THETA = 500000.0
NCORES = 8


class T:
    __slots__ = ("t", "w", "r", "name", "psum")

    def __init__(self, t, name="", psum=False):
        self.t = t
        self.w = None
        self.r = {}
        self.name = name
        self.psum = psum

    def __getitem__(self, idx):
        return self.t[idx]


class E:
    def __init__(self, name, h, sem):
        self.name = name
        self.h = h
        self.sem = sem
        self.cnt = 0
        self.seen = {}


class K:
    def __init__(self, nc, es, n_dma_sems=48):
        self.nc = nc
        self.es = es
        self.sems = {}
        self.eng = {}
        for name, h in (("pe", nc.tensor), ("act", nc.scalar), ("dve", nc.vector),
                        ("pool", nc.gpsimd), ("sp", nc.sync)):
            s = es.enter_context(nc.semaphore("s_" + name))
            self.sems[name] = s
            self.eng[name] = E(name, h, s)
        self.dma = {}
        self.dma_rr = {}
        for q, n in (("sp", n_dma_sems), ("pool", 24), ("act", 8)):
            self.dma[q] = []
            self.dma_rr[q] = 0
            for i in range(n):
                key = "dma_%s%d" % (q, i)
                self.sems[key] = es.enter_context(nc.semaphore("s_" + key))
                self.dma[q].append([key, 0])
        self.pending = {}
        self.n_inst = 0
        self.sems["cc"] = es.enter_context(nc.semaphore("s_cc"))
        self.cc_cnt = 0

    def sb(self, name, shape, dt):
        return T(self.es.enter_context(self.nc.sbuf_tensor(name, shape, dt)), name)

    def ps(self, name, shape, dt):
        return T(self.es.enter_context(self.nc.psum_tensor(name, shape, dt)), name, psum=True)

    def _wait(self, e, key, val):
        if val <= 0 or e.seen.get(key, 0) >= val:
            return
        e.h.wait_ge(self.sems[key], val)
        e.seen[key] = val

    def _deps(self, e, reads, writes):
        need = {}
        for t in reads:
            if t.w is not None:
                need[t.w[0]] = max(need.get(t.w[0], 0), t.w[1])
            if t.psum:
                for kk, v in t.r.items():
                    if kk != e.name:
                        need[kk] = max(need.get(kk, 0), v)
        for t in writes:
            if t.w is not None:
                need[t.w[0]] = max(need.get(t.w[0], 0), t.w[1])
            for kk, v in t.r.items():
                need[kk] = max(need.get(kk, 0), v)
        for kk, v in need.items():
            if kk == "pe" and e.name == "pe":
                continue
            self._wait(e, kk, v)

    def op(self, en, fn, reads=(), writes=()):
        e = self.eng[en]
        self._deps(e, reads, writes)
        inst = fn(e.h)
        e.cnt += 1
        inst.then_inc(e.sem, 1)
        for t in reads:
            t.r[en] = e.cnt
        for t in writes:
            t.w = (en, e.cnt)
            t.r = {}
        self.n_inst += 1
        return inst

    def dma_op(self, en, fn, reads=(), writes=(), out=False):
        e = self.eng[en]
        slot = self.dma[en][self.dma_rr[en] % len(self.dma[en])]
        self.dma_rr[en] += 1
        key, val = slot
        self._wait(e, key, val)
        self._deps(e, reads, writes)
        inst = fn(e.h)
        slot[1] = val + 16
        inst.then_inc(self.sems[key], 16)
        for t in reads:
            t.r[key] = slot[1]
        for t in writes:
            t.w = (key, slot[1])
            t.r = {}
        if out:
            self.pending[key] = slot[1]
        self.n_inst += 1
        return (key, slot[1])

    def cc_op(self, fn, reads=(), writes=()):
        e = self.eng["pool"]
        self._deps(e, reads, writes)
        inst = fn(e.h)
        self.cc_cnt += 1
        inst.then_inc(self.sems["cc"])
        for t in reads:
            t.r["cc"] = self.cc_cnt
        for t in writes:
            t.w = ("cc", self.cc_cnt)
            t.r = {}

    def barrier(self):
        for e in self.eng.values():
            for o in self.eng.values():
                if o is not e:
                    self._wait(e, o.name, o.cnt)
            for q in self.dma.values():
                for key, val in q:
                    self._wait(e, key, val)
            self._wait(e, "cc", self.cc_cnt)

    def finish(self, en="sp"):
        e = self.eng[en]
        for key, val in self.pending.items():
            self._wait(e, key, val)


def build_program(S, POOLN=2560):
    TT = S // 128
    nc = bass.Bass("TRN2", target_bir_lowering=False)
    dt = nc.dram_tensor
    xp = dt("xp", [S, D], F32, kind="ExternalInput")
    xs = dt("xs", [16, D], F32, kind="ExternalInput")
    w_mine = dt("w_mine", [2, D, 512], F32, kind="ExternalInput")
    w_pairs = dt("w_pairs", [8, D, 512], F32, kind="ExternalInput")
    a_norm = dt("a_norm", [1, D], F32, kind="ExternalInput")
    a_qk = dt("a_qk", [1, 256], F32, kind="ExternalInput")
    rope_p = dt("rope_p", [S, 32], F32, kind="ExternalInput")
    rope_s = dt("rope_s", [16, 32], F32, kind="ExternalInput")
    kvp = dt("kvp", [S, 2, 4, HD], F32, kind="ExternalOutput")
    kvs = dt("kvs", [16, 2, NH, HD], F32, kind="ExternalOutput")
    NB = S // 256
    NG = S // 512
    CH = min(S, 2048)
    NCH = S // CH
    og_in = [dt("og_in%d" % i, [256, CH], BF16) for i in range(NCH)]
    og_all = [dt("og_all%d" % i, [1024, CH], BF16) for i in range(NCH)]
    a_w_out = dt("a_w_out", [D, D], F32, kind="ExternalInput")
    b_norm = dt("b_norm", [1, D], F32, kind="ExternalInput")
    b_gk = dt("b_gk", [1, 256], F32, kind="ExternalInput")
    w_nkv = dt("w_nkv", [D, 512], F32, kind="ExternalInput")
    y0d = dt("y0d", [S, D], F32)
    nkv = dt("nkv", [S, 4, HD], F32, kind="ExternalOutput")
    WL = min(512, S)
    nwin = dt("nwin", [WL, 2, HD], F32, kind="ExternalOutput")
    SQ = S // 4
    NSA_ON = not os.environ.get("KNONSA")
    yp = dt("yp", [S if NSA_ON else SQ, D], F32, kind="ExternalOutput")
    ysm = dt("ysm", [16, D], F32, kind="ExternalOutput")
    nkvs = dt("nkvs", [16, 4, 4, HD], F32, kind="ExternalOutput")
    win_in = dt("win_in", [4, 512, 2, 4, HD], F32, kind="ExternalInput")
    nwins = dt("nwins", [4, 512, 2, 4, HD], F32, kind="ExternalOutput")
    kv6 = dt("kv6", [S, 6, HD], F32)
    kv6s = dt("kv6s", [4, 128, 6, HD], F32)
    y0s_d = dt("y0s_d", [128, D], F32)
    w_nq_all = dt("w_nq_all", [4, D, 512], F32, kind="ExternalInput")
    w_ng_all = dt("w_ng_all", [4, D, 16], F32, kind="ExternalInput")
    cache_n = dt("cache_n", [POOLN * 128, 16 * HD], F32, kind="ExternalInput")
    w_nq = dt("w_nq", [D, 512], F32, kind="ExternalInput")
    b_gq = dt("b_gq", [1, 256], F32, kind="ExternalInput")
    w_ng = dt("w_ng", [D, 16], F32, kind="ExternalInput")
    b_w_out = dt("b_w_out", [D, D], F32, kind="ExternalInput")
    c_w1 = dt("c_w1", [2, 2048, 128], F32, kind="ExternalInput")
    c_w2 = dt("c_w2", [2, 128, HD], F32, kind="ExternalInput")
    c_b1 = dt("c_b1", [2, 128, 1], F32, kind="ExternalInput")
    c_peT = dt("c_peT", [2, HD, 32], F32, kind="ExternalInput")
    c_g0 = dt("c_g0", [1, HD], F32, kind="ExternalInput")
    NSA = not os.environ.get("KNONSA")
    if NSA and bool(os.environ.get("KDEBUG")):
        ow_dbg = dt("ow_dbg", [S, 3, 4, HD], F32, kind="ExternalOutput")
        cmp_dbg = dt("cmp_dbg", [128, 512], BF16, kind="ExternalOutput")
    NPG = 64
    cache_m = dt("cache_m", [POOLN * 128, 2 * NH * HD], F32, kind="ExternalInput")
    ptab = dt("ptab", [1, 4 * NPG], I32, kind="ExternalInput")
    w_nkv_all = dt("w_nkv_all", [4, D, 512], F32, kind="ExternalInput")
    mskd = [dt("mskd%d" % i, [64, 33], F32) for i in range(4)]
    DEBUG = bool(os.environ.get("KDEBUG"))
    if DEBUG:
        ogs_dbg = dt("ogs_dbg", [128, 8, 128], BF16, kind="ExternalOutput")
    if DEBUG:
        og_dbg = dt("og_dbg", [256, S], BF16, kind="ExternalOutput")

    with ExitStack() as es:
        k = K(nc, es)
        identf = k.sb("identf", [128, 128], F32)
        identb = k.sb("identb", [128, 128], BF16)
        gbc = k.sb("gbc", [128, D], F32)
        gqk = k.sb("gqk", [128, 4, HD], F32)
        cs_p = k.sb("cs_p", [128, TT, 32], F32)
        cs_s = k.sb("cs_s", [128, 1, 32], F32)
        epsb = k.sb("epsb", [128, 1], F32)
        k.op("pool", lambda h: h.memset(epsb[:], EPS), writes=[epsb])
        k.op("pool", lambda h: h.memset(identf[:], 0.0), writes=[identf])
        k.op("pool", lambda h: h.affine_select(out=identf[:], in_=identf[:], pattern=[[-1, 128]],
                                               compare_op=ALU.not_equal, fill=1.0, base=0,
                                               channel_multiplier=1), reads=[identf], writes=[identf])
        k.op("dve", lambda h: h.tensor_copy(out=identb[:], in_=identf[:]), reads=[identf], writes=[identb])
        k.dma_op("sp", lambda h: h.dma_start(out=gbc[:], in_=a_norm[:, :].partition_broadcast(128)), writes=[gbc])
        k.dma_op("sp", lambda h: h.dma_start(out=gqk[:].rearrange("p h d -> p (h d)"),
                                             in_=a_qk[:, :].partition_broadcast(128)), writes=[gqk])
        k.dma_op("sp", lambda h: h.dma_start(out=cs_p[:], in_=rope_p.ap().rearrange("(t p) c -> p t c", p=128)),
                 writes=[cs_p])
        k.op("pool", lambda h: h.memset(cs_s[:], 0.0), writes=[cs_s])
        k.dma_op("sp", lambda h: h.dma_start(out=cs_s[0:16, 0, :], in_=rope_s[:, :]), writes=[cs_s])
        k.op("dve", lambda h: h.tensor_scalar(out=gqk[:, 0:2, :], in0=gqk[:, 0:2, :], scalar1=0.125, scalar2=None,
                                              op0=ALU.mult), reads=[gqk], writes=[gqk])

        banks = [k.ps("bank%d" % i, [128, 512], F32) for i in range(8)]

        def dbl(name, shape, dtp):
            return [k.sb("%s%d" % (name, i), shape, dtp) for i in range(2)]
        xt = dbl("xt", [128, D], F32)
        junk = dbl("junk", [128, D], BF16)
        ss = dbl("ss", [128, 1], F32)
        lnv = dbl("lnv", [128, 1], F32)
        rstd = dbl("rstd", [128, 1], F32)
        xn = dbl("xn", [128, D], BF16)
        xnT = dbl("xnT", [128, D], BF16)
        sq = dbl("sq", [128, 256], F32)
        hs = dbl("hs", [128, 4], F32)
        hl = dbl("hl", [128, 4], F32)
        hr = dbl("hr", [128, 4], F32)
        t1 = dbl("t1", [128, 4, HD], F32)
        qk = dbl("qk", [128, 4, HD], F32)
        rt = dbl("rt", [128, 4, 32], F32)
        vf = dbl("vf", [128, 128], F32)
        qkb = dbl("qkb", [128, 2, 3, HD], BF16)
        wb = dbl("wb", [128, 8, 512], BF16)
        QA = [k.sb("QA%d" % i, [128, S], BF16) for i in range(2)]
        KA = [k.sb("KA%d" % i, [128, S], BF16) for i in range(2)]
        kmT = k.sb("kmT", [128, 32], F32)
        kmTb = k.sb("kmTb", [128, 32], BF16)
        scb = dbl("scb", [128, 32], F32)
        mx8 = dbl("mx8", [128, 8], F32)
        Bpad = dbl("Bpad", [128, 128], BF16)
        pTs = dbl("pTs", [128, 512], BF16)
        rinv = dbl("rinv", [128, 1], F32)
        ogT = k.sb("ogT", [128, S], BF16)
        Cdf = k.sb("Cdf", [128, 512], F32)
        Cdiag = k.sb("Cdiag", [128, 4, 512], BF16)
        Vx = k.sb("Vx", [128, TT, 2, HD + 1], BF16)
        zs = k.sb("zs", [128, TT, 128], BF16)
        xnT_s = k.sb("xnT_s", [128, D], BF16)
        q_s = k.sb("q_s", [128, NH, HD], F32)
        k_s = k.sb("k_s", [128, NH, HD], F32)
        v_s = k.sb("v_s", [128, NH, HD], F32)
        z_s = k.sb("z_s", [128, NH, HD], F32)

        state = {"pj": 0}

        wstage = k.sb("wstage", [128, 4, 512], F32)

        def load_w(buf, src):
            for hf in range(2):
                k.dma_op("sp", lambda h, hf=hf: h.dma_start(
                    out=wstage[:], in_=src[hf * 512:(hf + 1) * 512, :].rearrange("(kc p) n -> p kc n", p=128)),
                    writes=[wstage])
                k.op("pool", lambda h, hf=hf: h.tensor_copy(out=buf[:, hf * 4:(hf + 1) * 4, :], in_=wstage[:]),
                     reads=[wstage], writes=[buf])

        def rms_and_transpose(NP, p, x_t, out_T, out_view, gain=None):
            gain = gain or gbc
            k.op("act", lambda h: h.activation(out=junk[p][0:NP, :], in_=x_t[0:NP, :], func=AF.Square,
                                               accum_out=ss[p][0:NP, :]),
                 reads=[x_t], writes=[junk[p], ss[p]])
            k.op("act", lambda h: h.activation(out=lnv[p][0:NP, :], in_=ss[p][0:NP, :], func=AF.Ln,
                                               scale=1.0 / D, bias=epsb[0:NP, :]), reads=[ss[p], epsb], writes=[lnv[p]])
            k.op("act", lambda h: h.activation(out=rstd[p][0:NP, :], in_=lnv[p][0:NP, :], func=AF.Exp,
                                               scale=-0.5), reads=[lnv[p]], writes=[rstd[p]])
            k.op("dve", lambda h: h.scalar_tensor_tensor(out=xn[p][0:NP, :], in0=x_t[0:NP, :],
                                                         scalar=rstd[p][0:NP, :], in1=gain[0:NP, :],
                                                         op0=ALU.mult, op1=ALU.mult),
                 reads=[x_t, rstd[p], gain], writes=[xn[p]])
            pT = banks[0 + p]
            pTb = pT[:].bitcast(BF16)
            for kc in range(8):
                k.op("pe", lambda h, kc=kc: h.transpose(out=pTb[:, kc * 128:kc * 128 + NP],
                                                        in_=xn[p][0:NP, kc * 128:(kc + 1) * 128],
                                                        identity=identb[0:NP, 0:NP]),
                     reads=[xn[p], identb], writes=[pT])
            if NP == 128:
                k.op("act", lambda h: h.activation(out=out_T[:], in_=pTb[:, :], func=AF.Copy),
                     reads=[pT], writes=[out_T])
            else:
                k.op("act", lambda h: h.activation(
                    out=out_T[:], in_=pTb[:, :].rearrange("p (kc n) -> p kc n", kc=8)[:, :, 0:NP], func=AF.Copy),
                    reads=[pT], writes=[out_T])

        def project_pair(NP, p, lhs_view, w_t, cs_view, dst, gains=None, nonorm=()):
            gains = gains or gqk
            pj = banks[2 + p]
            for kc in range(8):
                k.op("pe", lambda h, kc=kc: h.matmul(pj[0:NP, :], lhsT=lhs_view(kc), rhs=w_t[:, kc, :],
                                                     start=(kc == 0), stop=(kc == 7)),
                     reads=[dst["lhs_tile"], w_t], writes=[pj])
            k.op("act", lambda h: h.activation(out=sq[p][0:NP, :], in_=pj[0:NP, 0:256], func=AF.Square),
                 reads=[pj], writes=[sq[p]])
            k.op("dve", lambda h: h.tensor_reduce(out=hs[p][0:NP, :],
                                                  in_=sq[p][0:NP, :].rearrange("p (h d) -> p h d", h=4),
                                                  axis=AX.X, op=ALU.add), reads=[sq[p]], writes=[hs[p]])
            k.op("act", lambda h: h.activation(out=hl[p][0:NP, :], in_=hs[p][0:NP, :], func=AF.Ln,
                                               scale=1.0 / HD, bias=epsb[0:NP, :]), reads=[hs[p], epsb], writes=[hl[p]])
            k.op("act", lambda h: h.activation(out=hr[p][0:NP, :], in_=hl[p][0:NP, :], func=AF.Exp,
                                               scale=-0.5), reads=[hl[p]], writes=[hr[p]])
            for i in nonorm:
                k.op("dve", lambda h, i=i: h.memset(hr[p][0:NP, i:i + 1], 1.0), reads=[hr[p]], writes=[hr[p]])
            k.op("dve", lambda h: h.tensor_tensor(
                out=t1[p][0:NP], in0=pj[0:NP, 0:256].rearrange("p (h d) -> p h d", h=4),
                in1=hr[p][0:NP, :].unsqueeze(2).to_broadcast([NP, 4, HD]), op=ALU.mult),
                reads=[pj, hr[p]], writes=[t1[p]])
            k.op("dve", lambda h: h.tensor_tensor(out=qk[p][0:NP], in0=t1[p][0:NP], in1=gains[0:NP], op=ALU.mult),
                 reads=[t1[p], gains], writes=[qk[p]])
            cosv = cs_view[:, 0:16].unsqueeze(1).to_broadcast([NP, 4, 16])
            sinv = cs_view[:, 16:32].unsqueeze(1).to_broadcast([NP, 4, 16])
            k.op("dve", lambda h: h.tensor_tensor(out=rt[p][0:NP, :, 0:16], in0=qk[p][0:NP, :, 0:16], in1=cosv,
                                                  op=ALU.mult), reads=[qk[p], dst["cs_tile"]], writes=[rt[p]])
            k.op("dve", lambda h: h.tensor_tensor(out=rt[p][0:NP, :, 16:32], in0=qk[p][0:NP, :, 0:16], in1=sinv,
                                                  op=ALU.mult), reads=[qk[p], dst["cs_tile"], rt[p]], writes=[rt[p]])
            k.op("dve", lambda h: h.tensor_tensor(out=qk[p][0:NP, :, 0:8], in0=rt[p][0:NP, :, 0:8],
                                                  in1=rt[p][0:NP, :, 24:32], op=ALU.subtract),
                 reads=[rt[p], qk[p]], writes=[qk[p]])
            k.op("dve", lambda h: h.tensor_tensor(out=qk[p][0:NP, :, 8:16], in0=rt[p][0:NP, :, 8:16],
                                                  in1=rt[p][0:NP, :, 16:24], op=ALU.add),
                 reads=[rt[p], qk[p]], writes=[qk[p]])
            dst["after"](pj)

        STAGE = int(os.environ.get("KSTAGE", "9"))

        def build_attention_consts():
            for i in range(2):
                k.op("pool", lambda h, i=i: h.memset(QA[i][64:128, :], 0.0), writes=[QA[i]])
                k.op("pool", lambda h, i=i: h.memset(KA[i][64:128, :], 0.0), writes=[KA[i]])
                k.op("pool", lambda h, i=i: h.affine_select(
                    out=KA[i][64:96, :].rearrange("p (n k) -> p n k", k=256),
                    in_=KA[i][64:96, :].rearrange("p (n k) -> p n k", k=256),
                    pattern=[[1, NB], [0, 256]], compare_op=ALU.not_equal, fill=1.0, base=0,
                    channel_multiplier=-1), reads=[KA[i]], writes=[KA[i]])
            k.op("pool", lambda h: h.memset(kmTb[:], 0.0), writes=[kmTb])
            for i in range(2):
                k.op("pool", lambda h, i=i: h.memset(Bpad[i][:], 0.0), writes=[Bpad[i]])
            for r in range(4):
                k.op("pool", lambda h: h.memset(Cdf[:], 0.0), writes=[Cdf])
                if r > 0:
                    k.op("pool", lambda h, r=r: h.memset(Cdf[:, 0:128 * r], -30000.0), reads=[Cdf], writes=[Cdf])
                k.op("pool", lambda h, r=r: h.affine_select(
                    out=Cdf[:, 128 * r:128 * (r + 1)], in_=Cdf[:, 128 * r:128 * (r + 1)], pattern=[[1, 128]],
                    compare_op=ALU.is_ge, fill=-30000.0, base=0, channel_multiplier=-1), reads=[Cdf], writes=[Cdf])
                k.op("pool", lambda h, r=r: h.tensor_copy(out=Cdiag[:, r, :], in_=Cdf[:]), reads=[Cdf], writes=[Cdiag])

        def moba_attention(hp):
            step = 16
            for a in range(0, TT, step):
                b_ = min(TT, a + step)
                k.op("act", lambda h, a=a, b_=b_: h.activation(out=zs[:, a:b_, :], in_=zs[:, a:b_, :], func=AF.Silu),
                     reads=[zs], writes=[zs])
            for hh in range(2):
                qa, ka = QA[hh], KA[hh]
                k.op("dve", lambda h: h.tensor_reduce(out=kmT[0:64, 0:NB],
                                                      in_=ka[0:64, :].rearrange("p (n k) -> p n k", k=256),
                                                      axis=AX.X, op=ALU.add), reads=[ka], writes=[kmT])
                k.op("dve", lambda h: h.tensor_scalar(out=kmTb[0:64, 0:NB], in0=kmT[0:64, 0:NB], scalar1=1.0 / 256,
                                                      scalar2=None, op0=ALU.mult), reads=[kmT], writes=[kmTb])
                for tt in range(TT):
                    p = tt % 2
                    cur = tt // 2
                    sbk = banks[6 + p]
                    if cur > 3:
                        k.op("pe", lambda h, tt=tt: h.matmul(sbk[:, 0:32], lhsT=qa[:, tt * 128:(tt + 1) * 128],
                                                            rhs=kmTb[:, 0:32], start=True, stop=True),
                             reads=[qa, kmTb], writes=[sbk])
                        k.op("dve", lambda h: h.memset(scb[p][:], -1e30), writes=[scb[p]])
                        k.op("dve", lambda h, cur=cur: h.tensor_copy(out=scb[p][:, 0:cur], in_=sbk[:, 0:cur]),
                             reads=[sbk, scb[p]], writes=[scb[p]])
                        k.op("dve", lambda h: h.max(out=mx8[p][:], in_=scb[p][:]), reads=[scb[p]], writes=[mx8[p]])
                        k.op("dve", lambda h: h.tensor_scalar(out=Bpad[p][:, 64:96], in0=scb[p][:],
                                                              scalar1=mx8[p][:, 2:3], scalar2=-30000.0,
                                                              op0=ALU.is_lt, op1=ALU.mult),
                             reads=[scb[p], mx8[p]], writes=[Bpad[p]])
                        k.op("dve", lambda h, cur=cur: h.memset(Bpad[p][:, 64 + cur:65 + cur], 0.0),
                             reads=[Bpad[p]], writes=[Bpad[p]])
                    else:
                        k.op("dve", lambda h: h.memset(Bpad[p][:, 64:96], 0.0), writes=[Bpad[p]])
                    pB = banks[4 + p]
                    pBb = pB[:].bitcast(BF16)
                    k.op("pe", lambda h: h.transpose(out=pBb[:, 0:128], in_=Bpad[p][:], identity=identb[:]),
                         reads=[Bpad[p], identb], writes=[pB])
                    k.op("act", lambda h, tt=tt: h.activation(out=qa[64:128, tt * 128:(tt + 1) * 128],
                                                              in_=pBb[64:128, 0:128], func=AF.Copy),
                         reads=[pB], writes=[qa])
                for G in range(NG):
                    q0 = G * 512
                    nt = 4 * G + 4
                    Ob = banks[2:6]

                    def qk_mm(t):
                        sb_ = banks[t % 2]
                        diag = t >= 4 * G
                        k.op("pe", lambda h: h.matmul(sb_[:, :], lhsT=ka[:, t * 128:(t + 1) * 128],
                                                      rhs=qa[:, q0:q0 + 512], start=True, stop=not diag),
                             reads=[ka, qa], writes=[sb_])
                        if diag:
                            k.op("pe", lambda h: h.matmul(sb_[:, :], lhsT=identb[:], rhs=Cdiag[:, t - 4 * G, :],
                                                          start=False, stop=True),
                                 reads=[identb, Cdiag], writes=[sb_])

                    def ex(t):
                        k.op("act", lambda h: h.activation(out=pTs[t % 2][:], in_=banks[t % 2][:, :], func=AF.Exp),
                             reads=[banks[t % 2]], writes=[pTs[t % 2]])

                    def pv(t):
                        for r2 in range(4):
                            if t - 4 * G > r2:
                                continue
                            k.op("pe", lambda h, r2=r2: h.matmul(Ob[r2][:, 0:HD + 1],
                                                                 lhsT=pTs[t % 2][:, r2 * 128:(r2 + 1) * 128],
                                                                 rhs=Vx[:, t, hh, :], start=(t == 0),
                                                                 stop=(t == 4 * G + r2)),
                                 reads=[pTs[t % 2], Vx], writes=[Ob[r2]])
                    qk_mm(0)
                    for t in range(nt):
                        if t + 1 < nt:
                            qk_mm(t + 1)
                        ex(t)
                        pv(t)
                    for r2 in range(4):
                        tt = 4 * G + r2
                        p = r2 % 2
                        k.op("dve", lambda h, r2=r2: h.reciprocal(out=rinv[p][:], in_=Ob[r2][:, HD:HD + 1]),
                             reads=[Ob[r2]], writes=[rinv[p]])
                        k.op("dve", lambda h, r2=r2, tt=tt: h.scalar_tensor_tensor(
                            out=zs[:, tt, hh * HD:(hh + 1) * HD], in0=Ob[r2][:, 0:HD], scalar=rinv[p][:],
                            in1=zs[:, tt, hh * HD:(hh + 1) * HD], op0=ALU.mult, op1=ALU.mult),
                            reads=[Ob[r2], rinv[p], zs], writes=[zs])
            for tt in range(TT):
                p = tt % 2
                pO = banks[6 + p]
                pOb = pO[:].bitcast(BF16)
                k.op("pe", lambda h, tt=tt: h.transpose(out=pOb[:, 0:128], in_=zs[:, tt, :], identity=identb[:]),
                     reads=[zs, identb], writes=[pO])
                k.op("act", lambda h, tt=tt: h.activation(out=ogT[:, tt * 128:(tt + 1) * 128], in_=pOb[:, 0:128],
                                                          func=AF.Copy), reads=[pO], writes=[ogT])
            for ci in range(NCH):
                k.dma_op("sp", lambda h, ci=ci: h.dma_start(out=og_in[ci][hp * 128:(hp + 1) * 128, :],
                                                            in_=ogT[:, ci * CH:(ci + 1) * CH]),
                         reads=[ogT], writes=[OGIN[ci]])
            if DEBUG:
                k.dma_op("sp", lambda h: h.dma_start(out=og_dbg[hp * 128:(hp + 1) * 128, :], in_=ogT[:]),
                         reads=[ogT], out=True)


        def sample_phase():
            k.barrier()
            dead = [QA[1], KA[1], zs, ogT, cs_p, gbc, Cdiag, Cdf, junk[0], junk[1], QA[0] if S < 8192 else None]
            free = []
            for t_ in dead + S8["dead"]:
                if t_ is None:
                    continue
                ap = t_[:]
                nd = len(ap.shape)
                if nd == 3:
                    ap = ap.rearrange("p a b -> p (a b)")
                elif nd == 4:
                    ap = ap.rearrange("p a b c -> p (a b c)")
                nbytes = ap.shape[1] * (2 if ap.dtype == BF16 else 4)
                free.append([ap, ap.dtype, nbytes, 0])

            def carve(name, shape, dtp):
                isz = 2 if dtp == BF16 else 4
                n = 1
                for d_ in shape[1:]:
                    n *= d_
                need = n * isz
                for f in free:
                    ap, adt, nbytes, used = f
                    if nbytes - used >= need:
                        v = ap.bitcast(dtp) if adt != dtp else ap
                        v = v[:, used // isz: used // isz + n]
                        f[3] = used + ((need + 31) // 32) * 32
                        if len(shape) == 3:
                            v = v.rearrange("p (a b) -> p a b", a=shape[1])
                        return T(v, name)
                return k.sb(name, list(shape), dtp)

            kvpg = [carve("kvpg%d" % i, [128, 2048], F32) for i in range(2)]
            OB = carve("OB", [128, 33, 64], F32)
            LB = carve("LB", [128, 33, 64], F32)
            Mb = carve("Mb", [128, 64, 33], F32)
            OLc = carve("OLc", [128, 4, 64], F32)
            QT_s = carve("QT_s", [128, 8, 128], BF16)
            zT_s = carve("zT_s", [128, 8, 128], BF16)
            KnT_s = carve("KnT_s", [128, 8, 128], BF16)
            Vn_s = carve("Vn_s", [128, 1024], BF16)
            QsAp = carve("QsAp", [128, 8, 128], BF16)
            ogT_s = carve("ogT_s", [128, 8, 128], BF16)
            tb16 = carve("tb16", [128, 1024], BF16)
            Pm = [carve("Pm%d" % i, [128, 64], BF16) for i in range(2)]
            sadd = carve("sadd", [128, 64], F32)
            kmS = carve("kmS", [128, 8, 32], F32)
            kmSb = carve("kmSb", [128, 8, 32], BF16)
            ksum = [carve("ksum%d" % i, [128, 8], F32) for i in range(2)]
            scS = carve("scS", [128, 32], F32)
            mxS = carve("mxS", [128, 8], F32)
            m32 = carve("m32", [128, 32], F32)
            msk = carve("msk", [128, 33], F32)
            CN = [carve("CN%d" % i, [128, 64], F32) for i in range(4)]
            OL = carve("OL", [128, 2, 64], F32)
            oT = carve("oT", [128, 64], F32)
            ptb = carve("ptb", [128, 256], I32)
            ptf = carve("ptf", [128, 256], F32)
            idx = carve("idx", [128, 256], I32)
            iop = carve("iop", [128, 1], F32)
            onesb = carve("onesb", [128, 128], BF16)
            KTp = xnT
            Vb = xn
            Wo_ = S8["Wo"]
            g2_, gnsa_ = S8["g2"], S8["gnsa"]

            k.dma_op("sp", lambda h: h.dma_start(out=ptb[:], in_=ptab[:, :].partition_broadcast(128)), writes=[ptb])
            k.op("pool", lambda h: h.iota(iop[:], pattern=[[0, 1]], base=0, channel_multiplier=1,
                                          allow_small_or_imprecise_dtypes=True), writes=[iop])
            k.op("dve", lambda h: h.tensor_copy(out=ptf[:], in_=ptb[:]), reads=[ptb], writes=[ptf])
            k.op("dve", lambda h: h.tensor_scalar(out=ptf[:], in0=ptf[:], scalar1=128.0, scalar2=iop[:, 0:1],
                                                  op0=ALU.mult, op1=ALU.add), reads=[ptf, iop], writes=[ptf])
            k.op("dve", lambda h: h.tensor_copy(out=idx[:], in_=ptf[:]), reads=[ptf], writes=[idx])
            k.op("pool", lambda h: h.memset(onesb[:], 1.0), writes=[onesb])

            def to_T(src_f32, dstT, silu=False):
                if silu:
                    k.op("act", lambda h: h.activation(out=tb16[:], in_=src_f32[:].rearrange("p h d -> p (h d)"),
                                                       func=AF.Silu), reads=[src_f32], writes=[tb16])
                else:
                    k.op("dve", lambda h: h.tensor_copy(out=tb16[:], in_=src_f32[:].rearrange("p h d -> p (h d)")),
                         reads=[src_f32], writes=[tb16])
                for half in range(2):
                    bk = banks[6 + half]
                    bkb = bk[:].bitcast(BF16)
                    for i in range(4):
                        kc = half * 4 + i
                        k.op("pe", lambda h, kc=kc, i=i: h.transpose(out=bkb[:, i * 128:(i + 1) * 128],
                                                                     in_=tb16[:, kc * 128:(kc + 1) * 128],
                                                                     identity=identb[:]),
                             reads=[tb16, identb], writes=[bk])
                    k.op("act", lambda h, half=half: h.activation(
                        out=dstT[:, half * 4:(half + 1) * 4, :].rearrange("p a b -> p (a b)"), in_=bkb[:, 0:512],
                        func=AF.Copy), reads=[bk], writes=[dstT])
            to_T(q_s, QT_s)
            to_T(k_s, KnT_s)
            to_T(z_s, zT_s, silu=True)
            k.op("dve", lambda h: h.tensor_copy(out=Vn_s[:], in_=v_s[:].rearrange("p h d -> p (h d)")),
                 reads=[v_s], writes=[Vn_s])
            for bb in range(4):
                k.op("pool", lambda h, bb=bb: h.memset(CN[bb][:], 0.0), writes=[CN[bb]])
                k.op("pool", lambda h, bb=bb: h.affine_select(
                    out=CN[bb][:], in_=CN[bb][:], pattern=[[0, 64]], compare_op=ALU.is_ge, fill=-30000.0,
                    base=-4 * bb, channel_multiplier=1), reads=[CN[bb]], writes=[CN[bb]])
                k.op("pool", lambda h, bb=bb: h.affine_select(
                    out=CN[bb][:].rearrange("p (a q) -> p a q", q=4), in_=CN[bb][:].rearrange("p (a q) -> p a q", q=4),
                    pattern=[[0, 16], [1, 4]], compare_op=ALU.is_ge, fill=-30000.0, base=4 * bb,
                    channel_multiplier=-1), reads=[CN[bb]], writes=[CN[bb]])
            k.op("pool", lambda h: h.memset(ogT_s[:], 0.0), writes=[ogT_s])

            def unit(bb, u, first, kt_tile, kt_view, v_tile, v_view, addmask=None):
                p = u % 2
                psS = banks[0 + p]
                for j in range(8):
                    k.op("pe", lambda h, j=j: h.matmul(psS[:, 8 * j:8 * j + 8], lhsT=kt_view(j),
                                                       rhs=QsAp[:, j, 8 * j:8 * j + 8], start=True, stop=True),
                         reads=[kt_tile, QsAp], writes=[psS])
                if addmask is not None:
                    k.op("dve", lambda h: h.tensor_tensor(out=sadd[:], in0=psS[:, 0:64], in1=addmask[:], op=ALU.add),
                         reads=[psS, addmask], writes=[sadd])
                    k.op("act", lambda h: h.activation(out=Pm[p][:], in_=sadd[:], func=AF.Exp),
                         reads=[sadd], writes=[Pm[p]])
                else:
                    k.op("act", lambda h: h.activation(out=Pm[p][:], in_=psS[:, 0:64], func=AF.Exp),
                         reads=[psS], writes=[Pm[p]])
                psO = banks[2 + p]
                for j in range(8):
                    k.op("pe", lambda h, j=j: h.matmul(psO[:, 8 * j:8 * j + 8], lhsT=v_view(j),
                                                       rhs=Pm[p][:, 8 * j:8 * j + 8], start=True, stop=True),
                         reads=[v_tile, Pm[p]], writes=[psO])
                k.op("pe", lambda h: h.matmul(psO[:, 64:128], lhsT=onesb[:], rhs=Pm[p][:, 0:64], start=True, stop=True),
                     reads=[onesb, Pm[p]], writes=[psO])
                n = u // 2 if u < 64 else 32
                if first:
                    k.op("dve", lambda h: h.tensor_copy(out=OB[:, n, :], in_=psO[:, 0:64]), reads=[psO], writes=[OB])
                    k.op("dve", lambda h: h.tensor_copy(out=LB[:, n, :], in_=psO[:, 64:128]), reads=[psO], writes=[LB])
                else:
                    k.op("dve", lambda h: h.tensor_tensor(out=OB[:, n, :], in0=psO[:, 0:64], in1=OB[:, n, :], op=ALU.add),
                         reads=[psO, OB], writes=[OB])
                    k.op("dve", lambda h: h.tensor_tensor(out=LB[:, n, :], in0=psO[:, 64:128], in1=LB[:, n, :], op=ALU.add),
                         reads=[psO, LB], writes=[LB])

            for bb in range(4):
                k.op("pool", lambda h: h.memset(QsAp[:], 0.0), writes=[QsAp])
                for j in range(8):
                    k.op("dve", lambda h, j=j: h.tensor_copy(out=QsAp[0:64, j, 8 * j:8 * j + 4],
                                                             in_=QT_s[0:64, j, 4 * bb:4 * bb + 4]),
                         reads=[QT_s, QsAp], writes=[QsAp])
                    k.op("dve", lambda h, j=j: h.tensor_copy(out=QsAp[64:128, j, 8 * j + 4:8 * j + 8],
                                                             in_=QT_s[64:128, j, 4 * bb:4 * bb + 4]),
                         reads=[QT_s, QsAp], writes=[QsAp])
                for u in range(64):
                    p = u % 2
                    pgt = kvpg[p]
                    k.dma_op("pool", lambda h, u=u: h.indirect_dma_start(
                        out=pgt[:], out_offset=None, in_=cache_m[:, :],
                        in_offset=bass.IndirectOffsetOnAxis(ap=idx[:, bb * 64 + u:bb * 64 + u + 1], axis=0)),
                        reads=[idx], writes=[pgt])
                    k.op("pool", lambda h: h.tensor_copy(out=Vb[p][:], in_=pgt[:, 1024:2048]), reads=[pgt], writes=[Vb[p]])
                    for half in range(2):
                        bk = banks[4 + half]
                        for i in range(4):
                            kc = half * 4 + i
                            k.op("pe", lambda h, kc=kc, i=i: h.transpose(out=bk[:, i * 128:(i + 1) * 128],
                                                                         in_=pgt[:, kc * 128:(kc + 1) * 128],
                                                                         identity=identf[:]),
                                 reads=[pgt, identf], writes=[bk])
                        k.op("act", lambda h, half=half: h.activation(out=KTp[p][:, half * 512:(half + 1) * 512],
                                                                      in_=bk[:, :], func=AF.Copy),
                             reads=[bk], writes=[KTp[p]])
                        k.op("dve", lambda h, half=half: h.tensor_reduce(
                            out=ksum[p][:, half * 4:(half + 1) * 4], in_=bk[:, :].rearrange("p (a b) -> p a b", a=4),
                            axis=AX.X, op=ALU.add), reads=[bk], writes=[ksum[p]])
                    n = u // 2
                    if u % 2 == 0:
                        k.op("dve", lambda h, n=n: h.tensor_copy(out=kmS[:, :, n], in_=ksum[p][:]),
                             reads=[ksum[p]], writes=[kmS])
                    else:
                        k.op("dve", lambda h, n=n: h.tensor_tensor(out=kmS[:, :, n], in0=kmS[:, :, n], in1=ksum[p][:],
                                                                   op=ALU.add), reads=[ksum[p], kmS], writes=[kmS])
                    unit(bb, u, u % 2 == 0, KTp[p], lambda j, p=p: KTp[p][:, j * 128:(j + 1) * 128],
                         Vb[p], lambda j, p=p: Vb[p][:, j * 128:(j + 1) * 128])
                unit(bb, 64, True, KnT_s, lambda j: KnT_s[:, j, :], Vn_s, lambda j: Vn_s[:, j * 128:(j + 1) * 128],
                     addmask=CN[bb])
                k.op("dve", lambda h: h.tensor_scalar(out=kmSb[:], in0=kmS[:], scalar1=1.0 / 256, scalar2=None,
                                                      op0=ALU.mult), reads=[kmS], writes=[kmSb])
                psB = banks[6]
                for j in range(8):
                    k.op("pe", lambda h, j=j: h.matmul(psB[:, 0:32], lhsT=QsAp[:, j, :], rhs=kmSb[:, j, :],
                                                       start=(j == 0), stop=(j == 7)),
                         reads=[QsAp, kmSb], writes=[psB])
                k.op("dve", lambda h: h.tensor_copy(out=scS[:], in_=psB[:, 0:32]), reads=[psB], writes=[scS])
                k.op("dve", lambda h: h.max(out=mxS[:], in_=scS[:]), reads=[scS], writes=[mxS])
                k.op("dve", lambda h: h.tensor_scalar(out=msk[:, 0:32], in0=scS[:], scalar1=mxS[:, 2:3], scalar2=None,
                                                      op0=ALU.is_ge), reads=[scS, mxS], writes=[msk])
                k.op("dve", lambda h: h.memset(msk[:, 32:33], 1.0), reads=[msk], writes=[msk])
                MD = T(mskd[bb])
                k.dma_op("sp", lambda h: h.dma_start(out=mskd[bb][:, :], in_=msk[0:64, :]), reads=[msk], writes=[MD])
                k.dma_op("sp", lambda h: h.dma_start(
                    out=Mb[:].rearrange("p c u -> p (c u)"),
                    in_=mskd[bb].ap().rearrange("c u -> (c u)").rearrange("(o n) -> o n", o=1).partition_broadcast(128)),
                    reads=[MD], writes=[Mb])
                mbv = Mb[:].rearrange("p c u -> p u c")
                for which, src in ((0, OB), (1, LB)):
                    for ch, (u0, u1) in enumerate(((0, 17), (17, 33))):
                        tv = kvpg[0][:, 0:(u1 - u0) * 64].rearrange("p (u c) -> p u c", c=64)
                        k.op("dve", lambda h, src=src, u0=u0, u1=u1, tv=tv: h.tensor_tensor(
                            out=tv, in0=src[:, u0:u1, :], in1=mbv[:, u0:u1, :], op=ALU.mult),
                            reads=[src, Mb], writes=[kvpg[0]])
                        k.op("dve", lambda h, which=which, ch=ch, tv=tv: h.tensor_reduce(
                            out=OLc[:, 2 * which + ch, :], in_=tv.rearrange("p u c -> p c u"), axis=AX.X, op=ALU.add),
                            reads=[kvpg[0]], writes=[OLc])
                    k.op("dve", lambda h, which=which: h.tensor_tensor(out=OL[:, which, :], in0=OLc[:, 2 * which, :],
                                                                       in1=OLc[:, 2 * which + 1, :], op=ALU.add),
                         reads=[OLc], writes=[OL])
                k.op("dve", lambda h: h.reciprocal(out=OL[:, 1, :], in_=OL[:, 1, :]), reads=[OL], writes=[OL])
                k.op("dve", lambda h: h.tensor_tensor(out=oT[:], in0=OL[:, 0, :], in1=OL[:, 1, :], op=ALU.mult),
                     reads=[OL], writes=[oT])
                ov = oT[:].rearrange("p (j c) -> p j c", c=8)
                k.op("dve", lambda h: h.tensor_tensor(out=ogT_s[0:64, :, 4 * bb:4 * bb + 4], in0=ov[0:64, :, 0:4],
                                                      in1=zT_s[0:64, :, 4 * bb:4 * bb + 4], op=ALU.mult),
                     reads=[oT, zT_s, ogT_s], writes=[ogT_s])
                k.op("dve", lambda h: h.tensor_tensor(out=ogT_s[64:128, :, 4 * bb:4 * bb + 4], in0=ov[64:128, :, 4:8],
                                                      in1=zT_s[64:128, :, 4 * bb:4 * bb + 4], op=ALU.mult),
                     reads=[oT, zT_s, ogT_s], writes=[ogT_s])
            if DEBUG:
                k.dma_op("sp", lambda h: h.dma_start(out=ogs_dbg[:, :, :], in_=ogT_s[:]), reads=[ogT_s], out=True)

            xs_t = xt[0]
            k.op("pool", lambda h: h.memset(xs_t[:], 0.0), writes=[xs_t])
            k.dma_op("sp", lambda h: h.dma_start(out=xs_t[0:16, :], in_=xs[:, :]), writes=[xs_t])
            for n in range(2):
                for kc in range(8):
                    k.op("pe", lambda h, n=n, kc=kc: h.matmul(banks[n][:, :], lhsT=ogT_s[:, kc, :],
                                                              rhs=Wo_[:, kc, n * 512:(n + 1) * 512],
                                                              start=(kc == 0), stop=(kc == 7)),
                         reads=[ogT_s, Wo_], writes=[banks[n]])
            y0s = xt[1]
            for n in range(2):
                k.op("dve", lambda h, n=n: h.tensor_tensor(out=y0s[:, n * 512:(n + 1) * 512],
                                                           in0=xs_t[:, n * 512:(n + 1) * 512], in1=banks[n][:, :],
                                                           op=ALU.add), reads=[xs_t, banks[n]], writes=[y0s])
            k.dma_op("sp", lambda h: h.dma_start(out=y0s_d[:, :], in_=y0s[:]), reads=[y0s], out=True)
            rms_and_transpose(128, 0, y0s, xnT[0], None, gain=g2_)
            osts = [carve("osts%d" % i, [128, 6, HD], F32) for i in range(2)]
            for g in range(4):
                p = g % 2
                load_w(wb[p], w_nkv_all[g])

                def after_sn(pj, g=g, p=p):
                    k.op("dve", lambda h: h.tensor_copy(out=osts[p][:, 0:6:2, :], in_=qk[p][:, 0:3, :]),
                         reads=[qk[p]], writes=[osts[p]])
                    k.op("act", lambda h: h.activation(out=osts[p][:, 1:6:2, :],
                                                       in_=pj[:, 256:448].rearrange("p (h d) -> p h d", h=3),
                                                       func=AF.Copy), reads=[pj], writes=[osts[p]])
                    k.dma_op("sp", lambda h: h.dma_start(out=nkvs[:, :, g, :], in_=osts[p][0:16, 0:4, :]),
                             reads=[osts[p]], out=True)
                    k.dma_op("sp", lambda h: h.dma_start(out=kv6s[g], in_=osts[p][:, :, :]), reads=[osts[p]], out=True)
                    for bb in range(4):
                        k.dma_op("sp", lambda h, bb=bb: h.dma_start(out=nwins[bb, 508:512, :, g, :],
                                                                    in_=osts[p][4 * bb:4 * bb + 4, 4:6, :]),
                                 reads=[osts[p]], out=True)
                project_pair(128, p, lambda kc: xnT[0][:, kc * 128:(kc + 1) * 128], wb[p], cs_s[:, 0, :],
                             {"lhs_tile": xnT[0], "cs_tile": cs_s, "after": after_sn}, gains=gnsa_, nonorm=(0, 3))

        def nsa_prompt_phase():
            k.barrier()
            dead = [QA[0], QA[1], KA[0], KA[1], zs, ogT, q_s, k_s, v_s, z_s, gbc, Cdiag, Cdf, junk[1],
                    xnT_s, Vx, xn[1], xnT[1], scb[0], scb[1]]
            free = []
            for t_ in dead + S8["dead"]:
                ap = t_[:]
                nd = len(ap.shape)
                if nd == 3:
                    ap = ap.rearrange("p a b -> p (a b)")
                elif nd == 4:
                    ap = ap.rearrange("p a b c -> p (a b c)")
                nbytes = ap.shape[1] * (2 if ap.dtype == BF16 else 4)
                free.append([ap, ap.dtype, nbytes, 0, t_ is Vx])

            def carve(name, shape, dtp):
                isz = 2 if dtp == BF16 else 4
                n = 1
                for d_ in shape[1:]:
                    n *= d_
                need = n * isz
                for f in free:
                    ap, adt, nbytes, used, nocast = f
                    if nocast and adt != dtp:
                        continue
                    if nbytes - used >= need:
                        v = ap.bitcast(dtp) if adt != dtp else ap
                        v = v[:, used // isz: used // isz + n]
                        f[3] = used + ((need + 31) // 32) * 32
                        if len(shape) == 3:
                            v = v.rearrange("p (a b) -> p a b", a=shape[1])
                        return T(v, name)
                return k.sb(name, list(shape), dtp)

            KCV = carve("KCV", [128, S], BF16)
            KSA = carve("KSA", [128, S], BF16)
            KWz = carve("KWz", [128, S], BF16)
            VS = carve("VS", [128, TT, HD + 1], BF16)
            VW = carve("VW", [128, TT, HD + 1], BF16)
            Qc = carve("Qc", [128, 512], BF16)
            t6 = [carve("t6_%d" % i, [128, 6, HD], F32) for i in range(2)]
            t6b = [carve("t6b_%d" % i, [128, 384], BF16) for i in range(2)]
            q5 = carve("q5", [128, 5, HD], BF16)
            ztl = carve("ztl", [128, 256], F32)
            Cf = carve("Cf", [128, 4, 128], F32)
            Ctri4 = carve("Ctri4", [128, 512], BF16)
            Cband4 = carve("Cband4", [128, 512], BF16)
            gq1 = carve("gq1", [128, 4, HD], F32)
            owt = carve("owt", [128, 12, HD], F32)
            rl = carve("rl", [128, 4], F32)
            g2_ = S8["g2"]
            g2n = carve("g2n", [128, D], F32)
            k.dma_op("sp", lambda h: h.dma_start(out=g2n[:], in_=b_norm[:, :].partition_broadcast(128)), writes=[g2n])
            k.dma_op("sp", lambda h: h.dma_start(out=gq1[:].rearrange("p h d -> p (h d)"),
                                                 in_=b_gq[:, :].partition_broadcast(128)), writes=[gq1])
            k.op("dve", lambda h: h.tensor_scalar(out=gq1[:], in0=gq1[:], scalar1=0.125, scalar2=None, op0=ALU.mult),
                 reads=[gq1], writes=[gq1])
            k.dma_op("sp", lambda h: h.dma_start(out=cs_p[:], in_=rope_p.ap().rearrange("(t p) c -> p t c", p=128)),
                     writes=[cs_p])
            wq = wb[0]
            load_w(wq, w_nq.ap())
            for t_ in (KSA, KWz):
                k.op("pool", lambda h, t_=t_: h.memset(t_[64:128, :], 0.0), writes=[t_])
            k.op("pool", lambda h: h.memset(Qc[64:128, :], 0.0), writes=[Qc])
            k.op("pool", lambda h: h.memset(VS[:, :, HD:HD + 1], 1.0), writes=[VS])
            k.op("pool", lambda h: h.memset(VW[:, :, HD:HD + 1], 1.0), writes=[VW])
            k.op("pool", lambda h: h.memset(Cf[:], 0.0), writes=[Cf])
            k.op("pool", lambda h: h.affine_select(out=Cf[:], in_=Cf[:], pattern=[[0, 4], [1, 128]],
                                                   compare_op=ALU.is_ge, fill=-30000.0, base=0, channel_multiplier=-1),
                 reads=[Cf], writes=[Cf])
            k.op("pool", lambda h: h.tensor_copy(out=Ctri4[:], in_=Cf[:].rearrange("p a b -> p (a b)")),
                 reads=[Cf], writes=[Ctri4])
            k.op("pool", lambda h: h.memset(Cf[:], 0.0), reads=[Cf], writes=[Cf])
            k.op("pool", lambda h: h.affine_select(out=Cf[:], in_=Cf[:], pattern=[[0, 4], [-1, 128]],
                                                   compare_op=ALU.is_gt, fill=-30000.0, base=0, channel_multiplier=1),
                 reads=[Cf], writes=[Cf])
            k.op("pool", lambda h: h.tensor_copy(out=Cband4[:], in_=Cf[:].rearrange("p a b -> p (a b)")),
                 reads=[Cf], writes=[Cband4])

            for tt in range(TT):
                p = tt % 2
                k.dma_op("sp", lambda h, tt=tt: h.dma_start(out=t6[p][:], in_=kv6[tt * 128:(tt + 1) * 128, :, :]),
                         writes=[t6[p]])
                k.op("dve", lambda h: h.tensor_copy(out=t6b[p][:], in_=t6[p][:].rearrange("p a d -> p (a d)")),
                     reads=[t6[p]], writes=[t6b[p]])
                bk = banks[6 + p]
                bkb = bk[:].bitcast(BF16)
                for i in range(3):
                    k.op("pe", lambda h, i=i: h.transpose(out=bkb[:, i * 128:(i + 1) * 128],
                                                          in_=t6b[p][:, i * 128:(i + 1) * 128], identity=identb[:]),
                         reads=[t6b[p], identb], writes=[bk])
                sl = slice(tt * 128, (tt + 1) * 128)
                k.op("act", lambda h: h.activation(out=KCV[:, sl], in_=bkb[:, 0:128], func=AF.Copy),
                     reads=[bk], writes=[KCV])
                k.op("act", lambda h: h.activation(out=KSA[0:64, sl], in_=bkb[0:64, 128:256], func=AF.Copy),
                     reads=[bk], writes=[KSA])
                k.op("act", lambda h: h.activation(out=KWz[0:64, sl], in_=bkb[0:64, 256:384], func=AF.Copy),
                     reads=[bk], writes=[KWz])
                k.op("dve", lambda h, tt=tt: h.tensor_copy(out=VS[:, tt, 0:HD], in_=t6[p][:, 3, :]),
                     reads=[t6[p]], writes=[VS])
                k.op("dve", lambda h, tt=tt: h.tensor_copy(out=VW[:, tt, 0:HD], in_=t6[p][:, 5, :]),
                     reads=[t6[p]], writes=[VW])


            NC = (S - 32) // 16 + 1
            NCH_ = (NC + 127) // 128
            NS = S // 64
            W1 = [carve("W1K", [128, 32, 128], BF16), carve("W1V", [128, 32, 128], BF16)]
            W2 = [carve("W2K", [128, HD], BF16), carve("W2V", [128, HD], BF16)]
            w2f = carve("w2f", [128, HD], F32)
            peT = carve("peT", [128, 32], F32)
            peTb = carve("peTb", [128, 32], BF16)
            bv = [carve("bvk", [128, 1], F32), carve("bvv", [128, 1], F32)]
            b1t = carve("b1t", [128, 1], F32)
            g0b = carve("g0b", [128, HD], F32)
            xg = carve("xg", [128, 512], F32)
            ug = carve("ug", [128, 512], F32)
            GT = carve("GT", [128, 512], BF16)
            KCMP = carve("KCMP", [128, 512], BF16)
            VCMP = carve("VCMP", [128, 4, HD + 1], BF16)
            kcn = carve("kcn", [128, 128], BF16)
            jk = carve("jk", [128, HD], F32)
            ssk = carve("ssk", [128, 1], F32)
            k.dma_op("sp", lambda h: h.dma_start(out=g0b[:], in_=c_g0[:, :].partition_broadcast(128)), writes=[g0b])
            k.op("pool", lambda h: h.memset(KCMP[:], 0.0), writes=[KCMP])
            k.op("pool", lambda h: h.memset(kcn[:], 0.0), writes=[kcn])
            k.op("pool", lambda h: h.memset(GT[:], 0.0), writes=[GT])
            k.op("pool", lambda h: h.memset(VCMP[:], 0.0), writes=[VCMP])
            k.op("pool", lambda h: h.memset(VCMP[:, :, HD:HD + 1], 1.0), reads=[VCMP], writes=[VCMP])
            ws16 = wstage[:].rearrange("p a (b c) -> p (a b) c", c=128)
            def do_compress(kv):
                ph = banks[2]
                for l in range(32):
                    k.op("pe", lambda h, kv=kv, l=l: h.matmul(ph[:, 0:NC], lhsT=W1[kv][:, l, :],
                                                              rhs=KCV[:, l:l + 16 * (NC - 1) + 1:16],
                                                              start=(l == 0), stop=(l == 31)),
                         reads=[W1[kv], KCV], writes=[ph])
                k.op("dve", lambda h, kv=kv: h.tensor_scalar(out=xg[:, 0:NC], in0=ph[:, 0:NC], scalar1=bv[kv][:, 0:1],
                                                             scalar2=None, op0=ALU.add), reads=[ph, bv[kv]], writes=[xg])
                k.op("dve", lambda h: h.tensor_tensor(out=ug[:, 0:NC], in0=xg[:, 0:NC], in1=xg[:, 0:NC], op=ALU.mult),
                     reads=[xg], writes=[ug])
                k.op("dve", lambda h: h.tensor_scalar(out=ug[:, 0:NC], in0=ug[:, 0:NC], scalar1=0.044715, scalar2=1.0,
                                                      op0=ALU.mult, op1=ALU.add), reads=[ug], writes=[ug])
                k.op("dve", lambda h: h.tensor_tensor(out=ug[:, 0:NC], in0=ug[:, 0:NC], in1=xg[:, 0:NC], op=ALU.mult),
                     reads=[ug, xg], writes=[ug])
                k.op("act", lambda h: h.activation(out=ug[:, 0:NC], in_=ug[:, 0:NC], func=AF.Tanh,
                                                   scale=0.7978845608028654), reads=[ug], writes=[ug])
                k.op("dve", lambda h: h.tensor_scalar(out=ug[:, 0:NC], in0=ug[:, 0:NC], scalar1=1.0, scalar2=0.5,
                                                      op0=ALU.add, op1=ALU.mult), reads=[ug], writes=[ug])
                k.op("dve", lambda h: h.tensor_tensor(out=GT[:, 0:NC], in0=ug[:, 0:NC], in1=xg[:, 0:NC], op=ALU.mult),
                     reads=[ug, xg, GT], writes=[GT])
                for wc in range(NCH_):
                    po = banks[3]
                    k.op("pe", lambda h, kv=kv, wc=wc: h.matmul(po[:, 0:HD], lhsT=GT[:, wc * 128:(wc + 1) * 128],
                                                                rhs=W2[kv][:], start=True, stop=True),
                         reads=[GT, W2[kv]], writes=[po])
                    if kv == 0:
                        k.op("act", lambda h: h.activation(out=jk[:], in_=po[:, 0:HD], func=AF.Square, accum_out=ssk[:]),
                             reads=[po], writes=[jk, ssk])
                        k.op("act", lambda h: h.activation(out=ssk[:], in_=ssk[:], func=AF.Ln, scale=1.0 / HD,
                                                           bias=epsb[:]), reads=[ssk, epsb], writes=[ssk])
                        k.op("act", lambda h: h.activation(out=ssk[:], in_=ssk[:], func=AF.Exp, scale=-0.5),
                             reads=[ssk], writes=[ssk])
                        k.op("dve", lambda h: h.scalar_tensor_tensor(out=kcn[:, 0:HD], in0=po[:, 0:HD], scalar=ssk[:],
                                                                     in1=g0b[:], op0=ALU.mult, op1=ALU.mult),
                             reads=[po, ssk, g0b, kcn], writes=[kcn])
                        pk = banks[7]
                        pkb = pk[:].bitcast(BF16)
                        k.op("pe", lambda h: h.transpose(out=pkb[:, 0:128], in_=kcn[:], identity=identb[:]),
                             reads=[kcn, identb], writes=[pk])
                        k.op("act", lambda h, wc=wc: h.activation(out=KCMP[0:64, wc * 128:(wc + 1) * 128],
                                                                  in_=pkb[0:64, 0:128], func=AF.Copy),
                             reads=[pk, KCMP], writes=[KCMP])
                    else:
                        k.op("act", lambda h, wc=wc: h.activation(out=VCMP[:, wc, 0:HD], in_=po[:, 0:HD], func=AF.Copy),
                             reads=[po, VCMP], writes=[VCMP])

            def load_compress_weights():
              for kv in range(2):
                r0 = 64 * kv
                k.op("pool", lambda h, kv=kv: h.memset(W1[kv][:], 0.0), writes=[W1[kv]])
                for hf in range(2):
                    k.dma_op("sp", lambda h, kv=kv, hf=hf, r0=r0: h.dma_start(
                        out=ws16[r0:r0 + 64, :, :],
                        in_=c_w1[kv, hf * 1024:(hf + 1) * 1024, :].rearrange("(l d) n -> d l n", d=64)), writes=[wstage])
                    k.op("pool", lambda h, kv=kv, hf=hf, r0=r0: h.tensor_copy(
                        out=W1[kv][r0:r0 + 64, hf * 16:(hf + 1) * 16, :], in_=ws16[r0:r0 + 64, :, :]),
                        reads=[wstage, W1[kv]], writes=[W1[kv]])
                k.dma_op("sp", lambda h, kv=kv: h.dma_start(out=w2f[:], in_=c_w2[kv]), writes=[w2f])
                k.op("dve", lambda h, kv=kv: h.tensor_copy(out=W2[kv][:], in_=w2f[:]), reads=[w2f], writes=[W2[kv]])
                k.op("pool", lambda h: h.memset(peT[:], 0.0), writes=[peT])
                k.dma_op("sp", lambda h, kv=kv, r0=r0: h.dma_start(out=peT[r0:r0 + 64, :], in_=c_peT[kv]), writes=[peT])
                k.op("dve", lambda h: h.tensor_copy(out=peTb[:], in_=peT[:]), reads=[peT], writes=[peTb])
                k.dma_op("sp", lambda h, kv=kv: h.dma_start(out=b1t[:], in_=c_b1[kv]), writes=[b1t])
                pb = banks[7]
                for l in range(32):
                    k.op("pe", lambda h, kv=kv, l=l: h.matmul(pb[:, 0:1], lhsT=W1[kv][:, l, :], rhs=peTb[:, l:l + 1],
                                                              start=(l == 0), stop=(l == 31)),
                         reads=[W1[kv], peTb], writes=[pb])
                k.op("dve", lambda h, kv=kv: h.tensor_tensor(out=bv[kv][:], in0=pb[:, 0:1], in1=b1t[:], op=ALU.add),
                     reads=[pb, b1t], writes=[bv[kv]])

            def compress_all():
                for kv in range(2):
                    do_compress(kv)

            load_compress_weights()
            compress_all()
            if DEBUG:
                k.dma_op("sp", lambda h: h.dma_start(out=cmp_dbg[:, :], in_=KCMP[:]), reads=[KCMP], out=True)

            onesb2 = carve("onesb2", [128, 128], BF16)
            k.op("pool", lambda h: h.memset(onesb2[:], 1.0), writes=[onesb2])
            wgf = carve("wgf", [128, 8, 16], F32)
            wgb = carve("wgb", [128, 8, 16], BF16)
            k.dma_op("sp", lambda h: h.dma_start(out=wgf[:], in_=w_ng.ap().rearrange("(kc p) n -> p kc n", p=128)),
                     writes=[wgf])
            k.op("dve", lambda h: h.tensor_copy(out=wgb[:], in_=wgf[:]), reads=[wgf], writes=[wgb])
            gsb = carve("gsb", [128, 16], F32)
            acc = carve("acc", [128, 4, HD], F32)
            tmpa = carve("tmpa", [128, 4, HD], F32)
            sgz = carve("sgz", [128, 256], F32)
            ogb2 = carve("ogb2", [128, 256], BF16)
            og2T = [carve("og2T%d" % i, [128, 2, 128], BF16) for i in range(2)]
            fillreg = nc.gpsimd.to_reg(-30000.0)
            PC = [carve("PC%d" % i, [128, 512], BF16) for i in range(4)]
            rLb = carve("rLb", [128, 512], F32)
            OVf = carve("OVf", [128, 4, 128], F32)
            OVb = carve("OVb", [128, 4, 128], BF16)
            Cmf = carve("Cmf", [128, 4, 128], F32)
            Cmb = [carve("Cmb%d" % i, [128, 512], BF16) for i in range(2)]
            impT = carve("impT", [128, 128], F32)
            impq = carve("impq", [128, 128], F32)
            imp2 = carve("imp2", [128, 128], F32)
            m8a = carve("m8a", [128, 8], F32)
            m8b = carve("m8b", [128, 8], F32)
            Bq = carve("Bq", [128, 128], BF16)
            Br = carve("Br", [128, 128], BF16)
            QAl = carve("QAl", [128, 512], BF16)
            QAh = carve("QAh", [128, 512], BF16)
            k.op("pool", lambda h: h.memset(OVf[:], 1.0), writes=[OVf])
            for wc in range(4):
                k.op("pool", lambda h, wc=wc: h.affine_select(out=OVf[:, wc, :], in_=OVf[:, wc, :], pattern=[[-4, 128]],
                                                              compare_op=ALU.is_ge, fill=0.0, base=128 * wc + 1,
                                                              channel_multiplier=1), reads=[OVf], writes=[OVf])
                k.op("pool", lambda h, wc=wc: h.affine_select(out=OVf[:, wc, :], in_=OVf[:, wc, :], pattern=[[4, 128]],
                                                              compare_op=ALU.is_ge, fill=0.0, base=3 - 128 * wc,
                                                              channel_multiplier=-1), reads=[OVf], writes=[OVf])
            k.op("pool", lambda h: h.tensor_copy(out=OVb[:], in_=OVf[:]), reads=[OVf], writes=[OVb])
            for half in range((S + 4095) // 4096):
                c0, c1 = half * 4096, min(S, (half + 1) * 4096)
                nbk = (c1 - c0) // 64
                k.op("pool", lambda h, c0=c0, c1=c1, nbk=nbk: h.affine_select(
                    out=KSA[64:128, c0:c1].rearrange("p (n k) -> p n k", k=64),
                    in_=KSA[64:128, c0:c1].rearrange("p (n k) -> p n k", k=64),
                    pattern=[[1, nbk], [0, 64]], compare_op=ALU.not_equal, fill=1.0, base=0,
                    channel_multiplier=-1), reads=[KSA], writes=[KSA])
            k.op("pool", lambda h: h.memset(QAl[:], 0.0), writes=[QAl])
            k.op("pool", lambda h: h.memset(QAh[:], 0.0), writes=[QAh])

            ST = [banks[2], banks[3]]
            Oc, Os, Ow = banks[4], banks[5], banks[6]
            def nsa_tile(tt, sm=None):
                y0l = xt[0]
                if sm is None:
                    k.dma_op("sp", lambda h, tt=tt: h.dma_start(out=y0l[:], in_=y0d[tt * 128:(tt + 1) * 128, :]),
                             writes=[y0l])
                else:
                    k.dma_op("sp", lambda h: h.dma_start(out=y0l[:], in_=y0s_d[:, :]), reads=[Y0S], writes=[y0l])
                rms_and_transpose(128, 0, y0l, xnT[0], None, gain=g2n)
                wgx = wgb if sm is None else sm["wg"]

                def after_q(pj, tt=tt):
                    k.op("dve", lambda h: h.tensor_copy(out=q5[:, 0:4, :], in_=qk[0][:]), reads=[qk[0]], writes=[q5])
                    k.op("dve", lambda h: h.tensor_copy(out=q5[:, 4, :], in_=qk[0][:, 0, :]), reads=[qk[0], q5], writes=[q5])
                    k.op("act", lambda h: h.activation(out=ztl[:], in_=pj[:, 256:512], func=AF.Copy),
                         reads=[pj], writes=[ztl])
                    pg_ = banks[1]
                    for kc in range(8):
                        k.op("pe", lambda h, kc=kc: h.matmul(pg_[:, 0:16], lhsT=xnT[0][:, kc * 128:(kc + 1) * 128],
                                                             rhs=wgx[:, kc, :], start=(kc == 0), stop=(kc == 7)),
                             reads=[xnT[0], wgx], writes=[pg_])
                    k.op("act", lambda h: h.activation(out=gsb[:], in_=pg_[:, 0:16], func=AF.Exp, scale=-1.0),
                         reads=[pg_], writes=[gsb])
                    k.op("dve", lambda h: h.tensor_scalar(out=gsb[:], in0=gsb[:], scalar1=1.0, scalar2=None, op0=ALU.add),
                         reads=[gsb], writes=[gsb])
                    k.op("dve", lambda h: h.reciprocal(out=gsb[:], in_=gsb[:]), reads=[gsb], writes=[gsb])
                    bq = banks[7]
                    bqb = bq[:].bitcast(BF16)
                    qf = q5[:].rearrange("p a d -> p (a d)")
                    for i in range(4):
                        k.op("pe", lambda h, i=i: h.transpose(out=bqb[:, i * 128:(i + 1) * 128],
                                                              in_=qf[:, i * 64:i * 64 + 128], identity=identb[:]),
                             reads=[q5, identb], writes=[bq])
                    k.op("act", lambda h: h.activation(out=Qc[0:64, :], in_=bqb[0:64, 0:512], func=AF.Copy),
                         reads=[bq], writes=[Qc])
                if sm is None:
                    project_pair(128, 0, lambda kc: xnT[0][:, kc * 128:(kc + 1) * 128], wq, cs_p[:, tt, :],
                                 {"lhs_tile": xnT[0], "cs_tile": cs_p, "after": after_q}, gains=gq1)
                else:
                    project_pair(128, 0, lambda kc: xnT[0][:, kc * 128:(kc + 1) * 128], sm["wq"], cs_s[:, 0, :],
                                 {"lhs_tile": xnT[0], "cs_tile": cs_s, "after": after_q}, gains=gq1)

                def branch(Ob, key_tiles, kt_tile, v_tile, rhs_of, extra_of, keep=None, new=None):
                    k.op("dve", lambda h: h.memset(Ob[:, 0:4 * (HD + 1)], 0.0), writes=[Ob])
                    for n_, t in enumerate(key_tiles):
                        st = ST[n_ % 2]
                        ex = extra_of(t)
                        rt_ = rhs_of(t)
                        if new is not None and t == 64:
                            ktT, ktA, vT, vA = new[0], new[0][:, :], new[1], new[1][:, 0, :]
                        else:
                            ktT, ktA, vT, vA = kt_tile, kt_tile[:, t * 128:(t + 1) * 128], v_tile, v_tile[:, t, :]
                        k.op("pe", lambda h, ktA=ktA, rt_=rt_: h.matmul(st[:, :], lhsT=ktA, rhs=rt_[:, :], start=True,
                                                                        stop=(ex is None)),
                             reads=[ktT, rt_], writes=[st])
                        if ex is not None:
                            k.op("pe", lambda h, ex=ex: h.matmul(st[:, :], lhsT=identb[:], rhs=ex[:], start=False,
                                                                 stop=True), reads=[identb, ex], writes=[st])
                        pt_ = keep[n_] if keep is not None else pTs[n_ % 2]
                        k.op("act", lambda h, pt_=pt_: h.activation(out=pt_[:], in_=st[:, :], func=AF.Exp),
                             reads=[st], writes=[pt_])
                        for hh in range(4):
                            k.op("pe", lambda h, hh=hh, vA=vA, pt_=pt_: h.matmul(
                                Ob[:, hh * (HD + 1):(hh + 1) * (HD + 1)], lhsT=pt_[:, hh * 128:(hh + 1) * 128],
                                rhs=vA, start=False, stop=False, skip_group_check=True),
                                reads=[pt_, vT], writes=[Ob])

                def finish_branch(Ob, slot):
                    ov = Ob[:, 0:4 * (HD + 1)].rearrange("p (a e) -> p a e", a=4)
                    k.op("dve", lambda h: h.tensor_scalar(out=rl[:], in0=ov[:, :, HD], scalar1=1e-30, scalar2=None,
                                                          op0=ALU.max), reads=[Ob], writes=[rl])
                    k.op("dve", lambda h: h.reciprocal(out=rl[:], in_=rl[:]), reads=[rl], writes=[rl])
                    k.op("dve", lambda h: h.tensor_tensor(out=owt[:, 4 * slot:4 * slot + 4, :], in0=ov[:, :, 0:HD],
                                                          in1=rl[:].unsqueeze(2).to_broadcast([128, 4, HD]),
                                                          op=ALU.mult), reads=[Ob, rl], writes=[owt])


                q_lo, q_hi = tt * 128, tt * 128 + 127
                cchunks = [wc for wc in range(NCH_) if 16 * (128 * wc) + 31 <= q_hi]
                masks = {} if sm is None else {NCH_ - 1: CmS}
                for wc in (cchunks if sm is None else []):
                    delta = 128 * tt - 2048 * wc - 31
                    if delta >= 2032:
                        continue
                    cb = Cmb[len(masks) % 2]
                    k.op("pool", lambda h: h.memset(Cmf[:], 0.0), writes=[Cmf])
                    k.op("pool", lambda h, delta=delta: h.affine_select(
                        out=Cmf[:], in_=Cmf[:], pattern=[[0, 4], [1, 128]], compare_op=ALU.is_ge, fill=fillreg,
                        base=delta, channel_multiplier=-16), reads=[Cmf], writes=[Cmf])
                    k.op("pool", lambda h, cb=cb: h.tensor_copy(out=cb[:], in_=Cmf[:].rearrange("p a b -> p (a b)")),
                         reads=[Cmf], writes=[cb])
                    masks[wc] = cb
                if cchunks:
                    branch(Oc, cchunks, KCMP, VCMP, lambda t: Qc, lambda t: masks.get(t), keep=PC)
                    finish_branch(Oc, 0)
                    pL = banks[7]
                    for n_, wc in enumerate(cchunks):
                        k.op("pe", lambda h, n_=n_: h.matmul(pL[:, :], lhsT=onesb2[:], rhs=PC[n_][:],
                                                             start=(n_ == 0), stop=(n_ == len(cchunks) - 1)),
                             reads=[onesb2, PC[n_]], writes=[pL])
                    k.op("dve", lambda h: h.tensor_scalar(out=rLb[:], in0=pL[:, :], scalar1=1e-30, scalar2=None,
                                                          op0=ALU.max), reads=[pL], writes=[rLb])
                    k.op("dve", lambda h: h.reciprocal(out=rLb[:], in_=rLb[:]), reads=[rLb], writes=[rLb])
                    for n_, wc in enumerate(cchunks):
                        k.op("dve", lambda h, n_=n_: h.tensor_tensor(out=PC[n_][:], in0=PC[n_][:], in1=rLb[:],
                                                                     op=ALU.mult), reads=[PC[n_], rLb], writes=[PC[n_]])
                    pI = banks[7]
                    tot = len(cchunks) * 4
                    ii = 0
                    for n_, wc in enumerate(cchunks):
                        for hh in range(4):
                            k.op("pe", lambda h, n_=n_, wc=wc, hh=hh, ii=ii: h.matmul(
                                pI[:, 0:128], lhsT=OVb[:, wc, :], rhs=PC[n_][:, hh * 128:(hh + 1) * 128],
                                start=(ii == 0), stop=(ii == tot - 1)), reads=[OVb, PC[n_]], writes=[pI])
                            ii += 1
                    k.op("dve", lambda h: h.tensor_copy(out=impT[:], in_=pI[:, 0:128]), reads=[pI], writes=[impT])
                    k.op("pe", lambda h: h.transpose(out=pI[:, 128:256], in_=impT[:], identity=identf[:]),
                         reads=[impT, identf], writes=[pI])
                    k.op("dve", lambda h: h.tensor_copy(out=impq[:], in_=pI[:, 128:256]), reads=[pI], writes=[impq])
                else:
                    k.op("dve", lambda h: h.memset(owt[:, 0:4, :], 0.0), writes=[owt])
                    k.op("dve", lambda h: h.memset(impq[:], 0.0), writes=[impq])
                if sm is not None:
                    for jf in (0, 127):
                        k.op("dve", lambda h, jf=jf: h.memset(impq[:, jf:jf + 1], 1e30), reads=[impq], writes=[impq])
                for hf in (range(2) if sm is None else []):
                    cur = 2 * tt + hf
                    rows = slice(64 * hf, 64 * hf + 64)
                    if cur + 1 < 128:
                        k.op("dve", lambda h, rows=rows, cur=cur: h.memset(impq[rows, cur + 1:128], -1e30),
                             reads=[impq], writes=[impq])
                    for jf in sorted(set([0, cur, max(cur - 1, 0)])):
                        k.op("dve", lambda h, rows=rows, jf=jf: h.memset(impq[rows, jf:jf + 1], 1e30),
                             reads=[impq], writes=[impq])
                k.op("dve", lambda h: h.max(out=m8a[:], in_=impq[:]), reads=[impq], writes=[m8a])
                k.op("dve", lambda h: h.match_replace(out=imp2[:], in_to_replace=m8a[:], in_values=impq[:],
                                                      imm_value=-1e30), reads=[m8a, impq], writes=[imp2])
                k.op("dve", lambda h: h.max(out=m8b[:], in_=imp2[:]), reads=[imp2], writes=[m8b])
                thc = 7 if sm is None else 6
                k.op("dve", lambda h: h.tensor_scalar(out=Bq[:], in0=impq[:], scalar1=m8b[:, thc:thc + 1], scalar2=-30000.0,
                                                      op0=ALU.is_lt, op1=ALU.mult), reads=[impq, m8b], writes=[Bq])
                k.op("dve", lambda h: h.tensor_copy(out=Br[:, 0:64], in_=Bq[:, 64:128]), reads=[Bq], writes=[Br])
                k.op("dve", lambda h: h.tensor_copy(out=Br[:, 64:128], in_=Bq[:, 0:64]), reads=[Bq, Br], writes=[Br])
                pBt = banks[7]
                pBtb = pBt[:].bitcast(BF16)
                k.op("pe", lambda h: h.transpose(out=pBtb[:, 0:128], in_=Br[:], identity=identb[:]),
                     reads=[Br, identb], writes=[pBt])
                k.op("pe", lambda h: h.transpose(out=pBtb[:, 128:256], in_=Bq[:], identity=identb[:]),
                     reads=[Bq, identb], writes=[pBt])
                k.op("act", lambda h: h.activation(out=QAl[0:64, :], in_=Qc[0:64, :], func=AF.Copy),
                     reads=[Qc, QAl], writes=[QAl])
                k.op("act", lambda h: h.activation(out=QAh[0:64, :], in_=Qc[0:64, :], func=AF.Copy),
                     reads=[Qc, QAh], writes=[QAh])
                for hh in range(4):
                    k.op("act", lambda h, hh=hh: h.activation(out=QAl[64:128, hh * 128:(hh + 1) * 128],
                                                              in_=pBtb[64:128, 0:128], func=AF.Copy),
                         reads=[pBt, QAl], writes=[QAl])
                    k.op("dve", lambda h, hh=hh: h.tensor_copy(out=QAh[64:128, hh * 128:(hh + 1) * 128],
                                                               in_=pBtb[64:128, 128:256]),
                         reads=[pBt, QAh], writes=[QAh])
                if sm is None:
                    branch(Os, list(range(tt + 1)), KSA, VS, lambda t: QAl if t < 32 else QAh,
                           lambda t, tt=tt: Ctri4 if t == tt else None)
                else:
                    branch(Os, list(range(65)), KSA, VS, lambda t: Qc if t == 64 else (QAl if t < 32 else QAh),
                           lambda t: sm["cns"] if t == 64 else None, new=(KSn, VSn))
                finish_branch(Os, 1)

                wt = list(range(max(0, tt - 4), tt + 1))
                if sm is None:
                    branch(Ow, wt, KWz, VW, lambda t: Qc,
                           lambda t, tt=tt: Ctri4 if t == tt else (Cband4 if t == tt - 4 else None))
                else:
                    branch(Ow, wt, KWz, VW, lambda t: Qc,
                           lambda t: sm["cns"] if t == 64 else (CbandS if t == 60 else None), new=(KWn, VWn))
                finish_branch(Ow, 2)
                gv = gsb[:, 0:12].rearrange("p (r c) -> p r c", c=3)
                for c_ in range(3):
                    dst = acc if c_ == 0 else tmpa
                    k.op("dve", lambda h, c_=c_, dst=dst: h.tensor_tensor(
                        out=dst[:], in0=owt[:, 4 * c_:4 * c_ + 4, :],
                        in1=gv[:, :, c_].unsqueeze(2).to_broadcast([128, 4, HD]), op=ALU.mult),
                        reads=[owt, gsb], writes=[dst])
                    if c_ > 0:
                        k.op("dve", lambda h: h.tensor_tensor(out=acc[:], in0=acc[:], in1=tmpa[:], op=ALU.add),
                             reads=[acc, tmpa], writes=[acc])
                k.op("act", lambda h: h.activation(out=sgz[:], in_=ztl[:], func=AF.Exp, scale=-1.0),
                     reads=[ztl], writes=[sgz])
                k.op("dve", lambda h: h.tensor_scalar(out=sgz[:], in0=sgz[:], scalar1=1.0, scalar2=None, op0=ALU.add),
                     reads=[sgz], writes=[sgz])
                k.op("dve", lambda h: h.reciprocal(out=sgz[:], in_=sgz[:]), reads=[sgz], writes=[sgz])
                k.op("dve", lambda h: h.tensor_tensor(out=sgz[:], in0=sgz[:], in1=ztl[:], op=ALU.mult),
                     reads=[sgz, ztl], writes=[sgz])
                k.op("dve", lambda h: h.tensor_tensor(out=ogb2[:], in0=sgz[:], in1=acc[:].rearrange("p a d -> p (a d)"),
                                                      op=ALU.mult), reads=[sgz, acc], writes=[ogb2])
                po2 = banks[7]
                po2b = po2[:].bitcast(BF16)
                o2 = og2T[tt % 2]
                for i in range(2):
                    k.op("pe", lambda h, i=i: h.transpose(out=po2b[:, i * 128:(i + 1) * 128],
                                                          in_=ogb2[:, i * 128:(i + 1) * 128], identity=identb[:]),
                         reads=[ogb2, identb], writes=[po2])
                k.op("act", lambda h, o2=o2: h.activation(out=o2[:].rearrange("p a b -> p (a b)"), in_=po2b[:, 0:256],
                                                          func=AF.Copy), reads=[po2], writes=[o2])
                if sm is not None:
                    bb_, g_ = sm["bb"], sm["g"]
                    k.op("dve", lambda h: h.tensor_copy(out=ogT_s2[:, 2 * g_:2 * g_ + 2, 4 * bb_:4 * bb_ + 4],
                                                        in_=o2[:, :, 4 * bb_:4 * bb_ + 4]),
                         reads=[o2, ogT_s2], writes=[ogT_s2])
                    return
                ci, co = (tt * 128) // CH, (tt * 128) % CH
                for i in range(2):
                    k.dma_op("sp", lambda h, i=i, ci=ci, co=co, o2=o2: h.dma_start(
                        out=og_in[ci][i * 128:(i + 1) * 128, co:co + 128], in_=o2[:, i, :]),
                        reads=[o2], writes=[OGIN[ci]])
                if DEBUG:
                    k.dma_op("sp", lambda h, tt=tt: h.dma_start(out=ow_dbg[tt * 128:(tt + 1) * 128].rearrange("t a h d -> t (a h) d"), in_=owt[:]),
                             reads=[owt], out=True)


            for tt in range(TT):
                nsa_tile(tt)

            k.barrier()
            csf = cs_p[:].rearrange("p a b -> p (a b)")
            if S >= 8192:
                free.append([csf[:, 1024:2048], F32, 4096, 0, False])
                free.append([wb[0][:].rearrange("p a b -> p (a b)"), BF16, 8192, 0, False])
            Y0S = T(y0s_d)
            KSn = carve("KSn", [128, 128], BF16)
            KWn = carve("KWn", [128, 128], BF16)
            VSn = carve("VSn", [128, 1, HD + 1], BF16)
            VWn = carve("VWn", [128, 1, HD + 1], BF16)
            CmS = carve("CmS", [128, 512], BF16)
            CbandS = carve("CbandS", [128, 512], BF16)
            CNS = [carve("CNS%d" % i, [128, 512], BF16) for i in range(4)]
            ogT_s2 = carve("ogT_s2", [128, 8, 128], BF16)
            wgfs = carve("wgfs", [128, 8, 16], F32)
            wgbs = carve("wgbs", [128, 8, 16], BF16)
            pgn = [T(csf[:, 0:1024], "pgn0"), xt[1]] if S >= 8192 else [carve("pgn%d" % i, [128, 1024], F32) for i in range(2)]
            p4b = [carve("p4b%d" % i, [128, 256], BF16) for i in range(2)]
            wn2 = [carve("wn2_%d" % i, [128, 2, HD], F32) for i in range(2)]
            ptb2 = carve("ptb2", [128, 256], I32)
            ptf2 = carve("ptf2", [128, 256], F32)
            idx2 = carve("idx2", [128, 256], I32)
            iop2 = carve("iop2", [128, 1], F32)
            k.dma_op("sp", lambda h: h.dma_start(out=ptb2[:], in_=ptab[:, :].partition_broadcast(128)), writes=[ptb2])
            k.op("pool", lambda h: h.iota(iop2[:], pattern=[[0, 1]], base=0, channel_multiplier=1,
                                          allow_small_or_imprecise_dtypes=True), writes=[iop2])
            k.op("dve", lambda h: h.tensor_copy(out=ptf2[:], in_=ptb2[:]), reads=[ptb2], writes=[ptf2])
            k.op("dve", lambda h: h.tensor_scalar(out=ptf2[:], in0=ptf2[:], scalar1=128.0, scalar2=iop2[:, 0:1],
                                                  op0=ALU.mult, op1=ALU.add), reads=[ptf2, iop2], writes=[ptf2])
            k.op("dve", lambda h: h.tensor_copy(out=idx2[:], in_=ptf2[:]), reads=[ptf2], writes=[idx2])
            k.op("pool", lambda h: h.memset(ogT_s2[:], 0.0), writes=[ogT_s2])
            for t_ in (KSn, KWn):
                k.op("pool", lambda h, t_=t_: h.memset(t_[:], 0.0), writes=[t_])
            for t_ in (VSn, VWn):
                k.op("pool", lambda h, t_=t_: h.memset(t_[:, :, HD:HD + 1], 1.0), writes=[t_])
            k.op("pool", lambda h: h.memset(Cf[:], 0.0), reads=[Cf], writes=[Cf])
            k.op("pool", lambda h: h.affine_select(out=Cf[:], in_=Cf[:], pattern=[[0, 4], [0, 128]],
                                                   compare_op=ALU.is_ge, fill=fillreg, base=126, channel_multiplier=-1),
                 reads=[Cf], writes=[Cf])
            k.op("pool", lambda h: h.tensor_copy(out=CmS[:], in_=Cf[:].rearrange("p a b -> p (a b)")),
                 reads=[Cf], writes=[CmS])
            k.op("pool", lambda h: h.memset(Cf[:], 0.0), reads=[Cf], writes=[Cf])
            k.op("pool", lambda h: h.affine_select(
                out=Cf[:].rearrange("p a (b c) -> p (a b) c", c=4), in_=Cf[:].rearrange("p a (b c) -> p (a b) c", c=4),
                pattern=[[0, 128], [-1, 4]], compare_op=ALU.is_gt, fill=fillreg, base=0, channel_multiplier=1),
                reads=[Cf], writes=[Cf])
            k.op("pool", lambda h: h.tensor_copy(out=CbandS[:], in_=Cf[:].rearrange("p a b -> p (a b)")),
                 reads=[Cf], writes=[CbandS])
            for bb in range(4):
                k.op("pool", lambda h: h.memset(Cf[:], 0.0), reads=[Cf], writes=[Cf])
                k.op("pool", lambda h, bb=bb: h.affine_select(out=Cf[:], in_=Cf[:], pattern=[[0, 4], [0, 128]],
                                                              compare_op=ALU.is_ge, fill=fillreg, base=-4 * bb,
                                                              channel_multiplier=1), reads=[Cf], writes=[Cf])
                k.op("pool", lambda h: h.affine_select(out=Cf[:], in_=Cf[:], pattern=[[0, 4], [1, 128]],
                                                       compare_op=ALU.is_ge, fill=fillreg, base=0,
                                                       channel_multiplier=-1), reads=[Cf], writes=[Cf])
                k.op("pool", lambda h, bb=bb: h.tensor_copy(out=CNS[bb][:], in_=Cf[:].rearrange("p a b -> p (a b)")),
                     reads=[Cf], writes=[CNS[bb]])
            wqs = wb[1]
            for bb in range(4):
                for g in range(4):
                    load_w(wqs, w_nq_all[g])
                    k.dma_op("sp", lambda h, g=g: h.dma_start(
                        out=wgfs[:], in_=w_ng_all[g].rearrange("(kc p) n -> p kc n", p=128)), writes=[wgfs])
                    k.op("dve", lambda h: h.tensor_copy(out=wgbs[:], in_=wgfs[:]), reads=[wgfs], writes=[wgbs])
                    for u in range(64):
                        p = u % 2
                        pg_t = pgn[p]
                        k.dma_op("pool", lambda h, u=u, pg_t=pg_t: h.indirect_dma_start(
                            out=pg_t[:], out_offset=None, in_=cache_n[:, :],
                            in_offset=bass.IndirectOffsetOnAxis(ap=idx2[:, bb * 64 + u:bb * 64 + u + 1], axis=0)),
                            reads=[idx2], writes=[pg_t])
                        pv4 = pg_t[:].rearrange("p (a g d) -> p a g d", a=4, g=4)
                        k.op("pool", lambda h, g=g, pv4=pv4: h.tensor_copy(
                            out=p4b[p][:].rearrange("p (a d) -> p a d", a=4), in_=pv4[:, :, g, :]),
                            reads=[pg_t], writes=[p4b[p]])
                        bk = banks[6 + p]
                        bkb = bk[:].bitcast(BF16)
                        for i in range(2):
                            k.op("pe", lambda h, i=i: h.transpose(out=bkb[:, i * 128:(i + 1) * 128],
                                                                  in_=p4b[p][:, i * 128:(i + 1) * 128], identity=identb[:]),
                                 reads=[p4b[p], identb], writes=[bk])
                        sl = slice(u * 128, (u + 1) * 128)
                        k.op("act", lambda h, sl=sl: h.activation(out=KCV[:, sl], in_=bkb[:, 0:128], func=AF.Copy),
                             reads=[bk], writes=[KCV])
                        k.op("act", lambda h, sl=sl: h.activation(out=KSA[0:64, sl], in_=bkb[0:64, 128:256], func=AF.Copy),
                             reads=[bk, KSA], writes=[KSA])
                        k.op("dve", lambda h, u=u, g=g, pv4=pv4: h.tensor_copy(out=VS[:, u, 0:HD], in_=pv4[:, 3, g, :]),
                             reads=[pg_t, VS], writes=[VS])
                    for i in range(4):
                        p = i % 2
                        k.dma_op("sp", lambda h, i=i, g=g: h.dma_start(out=wn2[p][:],
                                                                       in_=win_in[bb, i * 128:(i + 1) * 128, :, g, :]),
                                 writes=[wn2[p]])
                        k.op("dve", lambda h: h.tensor_copy(out=p4b[p][:, 0:128], in_=wn2[p][:].rearrange("p a d -> p (a d)")),
                             reads=[wn2[p]], writes=[p4b[p]])
                        bk = banks[6 + p]
                        bkb = bk[:].bitcast(BF16)
                        k.op("pe", lambda h: h.transpose(out=bkb[:, 0:128], in_=p4b[p][:, 0:128], identity=identb[:]),
                             reads=[p4b[p], identb], writes=[bk])
                        sl = slice((60 + i) * 128, (61 + i) * 128)
                        k.op("act", lambda h, sl=sl: h.activation(out=KWz[0:64, sl], in_=bkb[0:64, 0:128], func=AF.Copy),
                             reads=[bk, KWz], writes=[KWz])
                        k.op("dve", lambda h, i=i: h.tensor_copy(out=VW[:, 60 + i, 0:HD], in_=wn2[p][:, 1, :]),
                             reads=[wn2[p], VW], writes=[VW])
                    KV6 = T(kv6s)
                    k.dma_op("sp", lambda h, g=g: h.dma_start(out=t6[0][:], in_=kv6s[g]), writes=[t6[0]])
                    k.op("dve", lambda h: h.tensor_copy(out=t6b[0][:], in_=t6[0][:].rearrange("p a d -> p (a d)")),
                         reads=[t6[0]], writes=[t6b[0]])
                    bk = banks[6]
                    bkb = bk[:].bitcast(BF16)
                    for i in (1, 2):
                        k.op("pe", lambda h, i=i: h.transpose(out=bkb[:, i * 128:(i + 1) * 128],
                                                              in_=t6b[0][:, i * 128:(i + 1) * 128], identity=identb[:]),
                             reads=[t6b[0], identb], writes=[bk])
                    k.op("act", lambda h: h.activation(out=KSn[0:64, :], in_=bkb[0:64, 128:256], func=AF.Copy),
                         reads=[bk, KSn], writes=[KSn])
                    k.op("act", lambda h: h.activation(out=KWn[0:64, :], in_=bkb[0:64, 256:384], func=AF.Copy),
                         reads=[bk, KWn], writes=[KWn])
                    k.op("dve", lambda h: h.tensor_copy(out=VSn[:, 0, 0:HD], in_=t6[0][:, 3, :]),
                         reads=[t6[0], VSn], writes=[VSn])
                    k.op("dve", lambda h: h.tensor_copy(out=VWn[:, 0, 0:HD], in_=t6[0][:, 5, :]),
                         reads=[t6[0], VWn], writes=[VWn])
                    compress_all()
                    nsa_tile(64, {"bb": bb, "g": g, "wq": wqs, "wg": wgbs, "cns": CNS[bb]})

            gather_og()
            k.barrier()
            Wo2 = S8["Wo"]
            ogc2 = S8["ogc"]
            for n in range(2):
                for hf in range(2):
                    k.dma_op("sp", lambda h, n=n, hf=hf: h.dma_start(
                        out=wstage[:], in_=b_w_out[hf * 512:(hf + 1) * 512, n * 512:(n + 1) * 512].rearrange(
                            "(kc p) n -> p kc n", p=128)), writes=[wstage])
                    k.op("pool", lambda h, n=n, hf=hf: h.tensor_copy(
                        out=Wo2[:, hf * 4:(hf + 1) * 4, n * 512:(n + 1) * 512], in_=wstage[:]),
                        reads=[wstage, Wo2], writes=[Wo2])
            xs2 = xt[0]
            k.dma_op("sp", lambda h: h.dma_start(out=xs2[:], in_=y0s_d[:, :]), reads=[Y0S], writes=[xs2])
            for n in range(2):
                for kc in range(8):
                    k.op("pe", lambda h, n=n, kc=kc: h.matmul(banks[n][:, :], lhsT=ogT_s2[:, kc, :],
                                                              rhs=Wo2[:, kc, n * 512:(n + 1) * 512],
                                                              start=(kc == 0), stop=(kc == 7)),
                         reads=[ogT_s2, Wo2], writes=[banks[n]])
            for n in range(2):
                k.op("dve", lambda h, n=n: h.tensor_tensor(out=xs2[:, n * 512:(n + 1) * 512],
                                                           in0=xs2[:, n * 512:(n + 1) * 512], in1=banks[n][:, :],
                                                           op=ALU.add), reads=[xs2, banks[n]], writes=[xs2])
            k.dma_op("sp", lambda h: h.dma_start(out=ysm[:, :], in_=xs2[0:16, :]), reads=[xs2], out=True)

            for tt in range(TT):
                p = tt % 2
                g, r = tt // 4, tt % 4
                if r == 0:
                    ci, co = (g * 512) // CH, (g * 512) % CH
                    k.dma_op("sp", lambda h, g=g, ci=ci, co=co: h.dma_start(
                        out=ogc2[g % 2][:], in_=og_all[ci][:, co:co + 512].rearrange("(kc p) t -> p kc t", p=128)),
                        reads=[OGALL[ci]], writes=[ogc2[g % 2]])
                k.dma_op("sp", lambda h, tt=tt, p=p: h.dma_start(out=xt[p][:], in_=y0d[tt * 128:(tt + 1) * 128, :]),
                         writes=[xt[p]])
                for n in range(2):
                    for kc in range(8):
                        k.op("pe", lambda h, n=n, kc=kc, g=g, r=r: h.matmul(
                            banks[n][:, :], lhsT=ogc2[g % 2][:, kc, r * 128:(r + 1) * 128],
                            rhs=Wo2[:, kc, n * 512:(n + 1) * 512], start=(kc == 0), stop=(kc == 7)),
                            reads=[ogc2[g % 2], Wo2], writes=[banks[n]])
                for n in range(2):
                    k.op("dve", lambda h, n=n, p=p: h.tensor_tensor(out=xt[p][:, n * 512:(n + 1) * 512],
                                                                    in0=xt[p][:, n * 512:(n + 1) * 512],
                                                                    in1=banks[n][:, :], op=ALU.add),
                         reads=[xt[p], banks[n]], writes=[xt[p]])
                k.dma_op("sp", lambda h, tt=tt, p=p: h.dma_start(out=yp[tt * 128:(tt + 1) * 128, :], in_=xt[p][:]),
                         reads=[xt[p]], out=True)

        S8 = {}
        OGIN = [T(t_) for t_ in og_in]
        OGALL = [T(t_) for t_ in og_all]

        def gather_og():
            for ci in range(NCH):
                k.cc_op(lambda h, ci=ci: h.collective_compute("AllGather", ALU.bypass,
                                                              replica_groups=[[0, 1, 2, 3], [4, 5, 6, 7]],
                                                              ins=[og_in[ci].ap().opt()], outs=[og_all[ci].ap().opt()]),
                        reads=[OGIN[ci]], writes=[OGALL[ci]])

        def out_proj_and_nsa_project():
            if S >= 8192:
                k.barrier()
                Wo = T(KA[0][:].rearrange("p (kc n) -> p kc n", kc=8), "Wo")
                ogc = [T(KA[1][:, i * 4096:(i + 1) * 4096].rearrange("p (kc n) -> p kc n", kc=8), "ogc%d" % i)
                       for i in range(2)]
                qf1 = QA[1][:].bitcast(F32)
                y0t = [T(qf1[:, i * D:(i + 1) * D], "y0t%d" % i) for i in range(2)]
                qf0 = QA[0][:].bitcast(F32)
                g2 = T(qf0[:, 0:D], "g2")
                gnsa = T(qf0[:, D:D + 256].rearrange("p (h d) -> p h d", h=4), "gnsa")
                ost = [T(qf0[:, 1280 + i * 384:1280 + (i + 1) * 384].rearrange("p (h d) -> p h d", h=6), "ost%d" % i)
                       for i in range(2)]
            else:
                Wo = k.sb("Wo", [128, 8, D], BF16)
                gnsa = k.sb("gnsa", [128, 4, HD], F32)
                g2 = k.sb("g2", [128, D], F32)
                ogc = dbl("ogc", [128, 8, 512], BF16)
                y0t = dbl("y0t", [128, D], F32)
                ost = dbl("ost", [128, 6, HD], F32)
            k.dma_op("sp", lambda h: h.dma_start(out=g2[:], in_=b_norm[:, :].partition_broadcast(128)), writes=[g2])
            k.dma_op("sp", lambda h: h.dma_start(out=gnsa[:].rearrange("p h d -> p (h d)"),
                                                 in_=b_gk[:, :].partition_broadcast(128)), writes=[gnsa])
            S8["Wo"], S8["g2"], S8["gnsa"] = Wo, g2, gnsa
            S8["ogc"] = ogc
            S8["dead"] = [] if S >= 8192 else [ogc[0], ogc[1], y0t[0], y0t[1]]
            for n in range(2):
                for hf in range(2):
                    k.dma_op("sp", lambda h, n=n, hf=hf: h.dma_start(
                        out=wstage[:], in_=a_w_out[hf * 512:(hf + 1) * 512, n * 512:(n + 1) * 512].rearrange(
                            "(kc p) n -> p kc n", p=128)), writes=[wstage])
                    k.op("pool", lambda h, n=n, hf=hf: h.tensor_copy(
                        out=Wo[:, hf * 4:(hf + 1) * 4, n * 512:(n + 1) * 512], in_=wstage[:]),
                        reads=[wstage], writes=[Wo])
            wn = wb[0]
            load_w(wn, w_nkv.ap())
            for tt in range(TT):
                p = tt % 2
                g, r = tt // 4, tt % 4
                if r == 0:
                    ci, co = (g * 512) // CH, (g * 512) % CH
                    k.dma_op("sp", lambda h, g=g, ci=ci, co=co: h.dma_start(
                        out=ogc[g % 2][:], in_=og_all[ci][:, co:co + 512].rearrange("(kc p) t -> p kc t", p=128)),
                        reads=[OGALL[ci]], writes=[ogc[g % 2]])
                k.dma_op("sp", lambda h, tt=tt, p=p: h.dma_start(out=xt[p][:], in_=xp[tt * 128:(tt + 1) * 128, :]),
                         writes=[xt[p]])
                for n in range(2):
                    for kc in range(8):
                        k.op("pe", lambda h, n=n, kc=kc: h.matmul(
                            banks[n][:, :], lhsT=ogc[g % 2][:, kc, r * 128:(r + 1) * 128],
                            rhs=Wo[:, kc, n * 512:(n + 1) * 512], start=(kc == 0), stop=(kc == 7)),
                            reads=[ogc[g % 2], Wo], writes=[banks[n]])
                for n in range(2):
                    k.op("dve", lambda h, n=n: h.tensor_tensor(out=y0t[p][:, n * 512:(n + 1) * 512],
                                                               in0=xt[p][:, n * 512:(n + 1) * 512],
                                                               in1=banks[n][:, :], op=ALU.add),
                         reads=[xt[p], banks[n]], writes=[y0t[p]])
                k.dma_op("sp", lambda h, tt=tt: h.dma_start(out=y0d[tt * 128:(tt + 1) * 128, :], in_=y0t[p][:]),
                         reads=[y0t[p]], out=True)
                rms_and_transpose(128, p, y0t[p], xnT[p], None, gain=g2)

                def after_n(pj, tt=tt, p=p):
                    k.op("dve", lambda h: h.tensor_copy(out=ost[p][:, 0:6:2, :], in_=qk[p][:, 0:3, :]),
                         reads=[qk[p]], writes=[ost[p]])
                    k.op("act", lambda h: h.activation(out=ost[p][:, 1:6:2, :],
                                                       in_=pj[:, 256:448].rearrange("p (h d) -> p h d", h=3),
                                                       func=AF.Copy), reads=[pj], writes=[ost[p]])
                    k.dma_op("sp", lambda h: h.dma_start(out=nkv[tt * 128:(tt + 1) * 128, :, :], in_=ost[p][:, 0:4, :]),
                             reads=[ost[p]], out=True)
                    k.dma_op("sp", lambda h: h.dma_start(out=kv6[tt * 128:(tt + 1) * 128, :, :], in_=ost[p][:, :, :]),
                             reads=[ost[p]], out=True)
                    if tt >= TT - WL // 128:
                        w0 = (tt - (TT - WL // 128)) * 128
                        k.dma_op("sp", lambda h: h.dma_start(out=nwin[w0:w0 + 128, :, :], in_=ost[p][:, 4:6, :]),
                                 reads=[ost[p]], out=True)
                project_pair(128, p, lambda kc, p=p: xnT[p][:, kc * 128:(kc + 1) * 128], wn, cs_p[:, tt, :],
                             {"lhs_tile": xnT[p], "cs_tile": cs_p, "after": after_n}, gains=gnsa, nonorm=(0, 3))

        xs_t = xt[0]
        k.op("pool", lambda h: h.memset(xs_t[:], 0.0), writes=[xs_t])
        k.dma_op("sp", lambda h: h.dma_start(out=xs_t[0:16, :], in_=xs[:, :]), writes=[xs_t])
        if STAGE >= 1:
            rms_and_transpose(128, 0, xs_t, xnT_s, None)

        for j in range(8 if (STAGE >= 2 and not os.environ.get('KSKIP2')) else 0):
            p = j % 2
            load_w(wb[p], w_pairs[j])

            def after_s(pj, j=j, p=p):
                k.op("dve", lambda h: h.tensor_copy(out=q_s[:, 2 * j:2 * j + 2, :], in_=qk[p][:, 0:2, :]),
                     reads=[qk[p]], writes=[q_s])
                k.op("dve", lambda h: h.tensor_copy(out=k_s[:, 2 * j:2 * j + 2, :], in_=qk[p][:, 2:4, :]),
                     reads=[qk[p]], writes=[k_s])
                k.op("act", lambda h: h.activation(
                    out=v_s[:, 2 * j:2 * j + 2, :].rearrange("p h d -> p (h d)"), in_=pj[:, 256:384],
                    func=AF.Copy), reads=[pj], writes=[v_s])
                k.op("act", lambda h: h.activation(
                    out=z_s[:, 2 * j:2 * j + 2, :].rearrange("p h d -> p (h d)"), in_=pj[:, 384:512],
                    func=AF.Copy), reads=[pj], writes=[z_s])
            project_pair(128, p, lambda kc: xnT_s[:, kc * 128:(kc + 1) * 128], wb[p], cs_s[:, 0, :],
                         {"lhs_tile": xnT_s, "cs_tile": cs_s, "after": after_s})
        k.dma_op("sp", lambda h: h.dma_start(out=kvs[:, 0, :, :], in_=k_s[0:16]), reads=[k_s], out=True)
        k.dma_op("sp", lambda h: h.dma_start(out=kvs[:, 1, :, :], in_=v_s[0:16]), reads=[v_s], out=True)

        for hp in range(2 if STAGE >= 3 else 0):
            wm = wb[hp]
            load_w(wm, w_mine[hp])
            if hp == 0:
                k.op("pool", lambda h: h.memset(Vx[:, :, :, HD:HD + 1], 1.0), writes=[Vx])
                build_attention_consts()
            for tt in range(TT):
                p = tt % 2
                k.dma_op("sp", lambda h, tt=tt, p=p: h.dma_start(out=xt[p][:], in_=xp[tt * 128:(tt + 1) * 128, :]),
                         writes=[xt[p]])
                rms_and_transpose(128, p, xt[p], xnT[p], None)

                def after_p(pj, tt=tt, p=p, hp=hp):
                    k.dma_op("sp", lambda h: h.dma_start(out=kvp[tt * 128:(tt + 1) * 128, 0, 2 * hp:2 * hp + 2, :],
                                                         in_=qk[p][:, 2:4, :]), reads=[qk[p]], out=True)
                    k.op("act", lambda h: h.activation(out=vf[p][:], in_=pj[:, 256:384], func=AF.Copy),
                         reads=[pj], writes=[vf[p]])
                    k.dma_op("sp", lambda h: h.dma_start(
                        out=kvp[tt * 128:(tt + 1) * 128, 1, 2 * hp:2 * hp + 2, :],
                        in_=vf[p][:].rearrange("p (h d) -> p h d", h=2)), reads=[vf[p]], out=True)
                    k.op("dve", lambda h: h.tensor_copy(out=Vx[:, tt, :, 0:HD],
                                                        in_=pj[:, 256:384].rearrange("p (h d) -> p h d", h=2)),
                         reads=[pj], writes=[Vx])
                    k.op("act", lambda h: h.activation(out=zs[:, tt, :], in_=pj[:, 384:512], func=AF.Copy),
                         reads=[pj], writes=[zs])
                    qk4 = qk[p][:].rearrange("p (a h) d -> p a h d", a=2)
                    k.op("dve", lambda h: h.tensor_copy(out=qkb[p][:, :, 0:2, :], in_=qk4), reads=[qk[p]], writes=[qkb[p]])
                    k.op("dve", lambda h: h.tensor_copy(out=qkb[p][:, :, 2, :], in_=qk4[:, :, 0, :]),
                         reads=[qk[p]], writes=[qkb[p]])
                    pT2 = banks[4 + p]
                    pT2b = pT2[:].bitcast(BF16)
                    qf = qkb[p][:].rearrange("p a h d -> p (a h d)")
                    for i, c0 in enumerate((0, 64, 192, 256)):
                        k.op("pe", lambda h, i=i, c0=c0: h.transpose(out=pT2b[:, i * 128:(i + 1) * 128],
                                                                     in_=qf[:, c0:c0 + 128], identity=identb[:]),
                             reads=[qkb[p], identb], writes=[pT2])
                    dsts = (QA[0], QA[1], KA[0], KA[1])
                    for i in range(4):
                        if i % 2 == 0:
                            k.op("act", lambda h, i=i: h.activation(out=dsts[i][0:64, tt * 128:(tt + 1) * 128],
                                                                    in_=pT2b[0:64, i * 128:(i + 1) * 128], func=AF.Copy),
                                 reads=[pT2], writes=[dsts[i]])
                        else:
                            k.op("dve", lambda h, i=i: h.tensor_copy(out=dsts[i][0:64, tt * 128:(tt + 1) * 128],
                                                                     in_=pT2b[0:64, i * 128:(i + 1) * 128]),
                                 reads=[pT2], writes=[dsts[i]])
                project_pair(128, p, lambda kc, p=p: xnT[p][:, kc * 128:(kc + 1) * 128], wm, cs_p[:, tt, :],
                             {"lhs_tile": xnT[p], "cs_tile": cs_p, "after": after_p})
            if STAGE >= 4:
                moba_attention(hp)
        if STAGE >= 5:
            gather_og()
            out_proj_and_nsa_project()
        if STAGE >= 6:
            sample_phase()
        if NSA:
            nsa_prompt_phase()
        for bb in range(4):
            k.dma_op("sp", lambda h, bb=bb: h.dma_start(out=nwins[bb, 0:508], in_=win_in[bb, 4:512]), out=True)
        zt = xt[0]
        k.op("pool", lambda h: h.memset(zt[:], 0.0), writes=[zt])
        for i in range(0 if NSA_ON else SQ // 128):
            k.dma_op("sp", lambda h, i=i: h.dma_start(out=yp[i * 128:(i + 1) * 128, :], in_=zt[:]), reads=[zt], out=True)
        if not NSA_ON:
            k.dma_op("sp", lambda h: h.dma_start(out=ysm[:, :], in_=zt[0:16, :]), reads=[zt], out=True)
        if STAGE < 6:
            k.dma_op("sp", lambda h: h.dma_start(out=nkvs[:].rearrange("t a g d -> t (a g d)"), in_=zt[0:16, :]),
                     reads=[zt], out=True)
            for bb in range(4):
                k.dma_op("sp", lambda h, bb=bb: h.dma_start(
                    out=nwins[bb, 508:512].rearrange("t a g d -> t (a g d)"), in_=zt[0:4, 0:512]), reads=[zt], out=True)
        k.finish("sp")
    return nc


def rope_table(pos):
    half = 8
    inv = np.float32(THETA) ** (-np.arange(half, dtype=np.float32) / np.float32(half))
    ang = pos.astype(np.float32)[:, None] * inv[None, :]
    c = np.cos(ang).astype(np.float32)
    s = np.sin(ang).astype(np.float32)
    return np.ascontiguousarray(np.concatenate([c, c, s, s], axis=1))


def pair_cols(j):
    return np.concatenate([np.arange(128) + 128 * j + 1024 * s for s in range(4)])


def make_inputs(c, inp, S, past_len, shared):
    b, hg = c // 4, c % 4
    w_pairs = shared["w_pairs"]
    gq, gk = inp["a_q_norm"][0], inp["a_k_norm"][0]
    a_qk = np.concatenate([gq, gq, gk, gk])[None, :].astype(np.float32)
    wn = inp["b_w_in"][0]
    z64 = np.zeros((D, HD), np.float32)
    col = lambda base: wn[:, base + 64 * hg: base + 64 * hg + 64]
    w_nkv = np.ascontiguousarray(np.concatenate(
        [col(1024), col(1536), col(2048), z64, col(1280), col(1792), col(2304), z64], axis=1))
    kg = inp["b_k_norm"][0]
    one = np.ones(HD, np.float32)
    b_gk = np.concatenate([one, kg[1], kg[2], one])[None, :].astype(np.float32)
    return {
        "xp": np.ascontiguousarray(inp["x_prompt"][b, :S]),
        "xs": np.ascontiguousarray(inp["x_sample"][4 * c:4 * c + 4].reshape(16, D)),
        "w_mine": np.ascontiguousarray(w_pairs[2 * hg:2 * hg + 2]), "w_pairs": w_pairs,
        "a_norm": np.ascontiguousarray(inp["a_norm"][0:1]),
        "a_qk": a_qk,
        "rope_p": shared["rope_p"],
        "rope_s": shared["rope_s"],
        "a_w_out": shared["a_w_out"],
        "b_norm": np.ascontiguousarray(inp["b_norm"][0:1]),
        "b_gk": b_gk,
        "w_nkv": w_nkv,
        "win_in": np.ascontiguousarray(inp["state_nsa_win"][0, 4 * c:4 * c + 4]),
        "cache_m": shared["cache_m"],
        "ptab": np.ascontiguousarray(inp["page_table"][4 * c:4 * c + 4].reshape(1, -1).astype(np.int32)),
        "w_nkv_all": shared["w_nkv_all"],
        "w_nq": np.ascontiguousarray(np.concatenate([wn[:, 256 * hg:256 * hg + 256],
                                                     wn[:, 2608 + 256 * hg:2608 + 256 * hg + 256]], axis=1)),
        "b_gq": np.tile(inp["b_q_norm"][0], 4)[None, :].astype(np.float32),
        "w_nq_all": shared["w_nq_all"],
        "w_ng_all": shared["w_ng_all"],
        "cache_n": shared["cache_n"],
        "w_ng": np.ascontiguousarray(np.concatenate([wn[:, 2560 + 12 * hg:2560 + 12 * hg + 12],
                                                     np.zeros((D, 4), np.float32)], axis=1)),
        "b_w_out": shared["b_w_out"],
        "c_w1": np.ascontiguousarray(inp["b_cmp_w1"][0]),
        "c_w2": np.ascontiguousarray(inp["b_cmp_w2"][0]),
        "c_b1": np.ascontiguousarray(inp["b_cmp_b1"][0][:, :, None]),
        "c_peT": np.ascontiguousarray(inp["b_cmp_pe"][0].transpose(0, 2, 1)),
        "c_g0": np.ascontiguousarray(inp["b_k_norm"][0][0:1]),
    }


_PROG = {}


def run(inputs, S, past_len):
    POOLN = inputs["cache_moba_kv"].shape[1]
    if (S, POOLN) not in _PROG:
        _PROG[(S, POOLN)] = build_program(S, POOLN)
    nc = _PROG[(S, POOLN)]
    w = inputs["a_w_in"][0]
    shared = {
        "w_pairs": np.ascontiguousarray(np.stack([w[:, pair_cols(j)] for j in range(8)])),
        "rope_p": rope_table(np.arange(S)),
        "rope_s": rope_table(past_len + (np.arange(16) % 4)),
        "a_w_out": np.ascontiguousarray(inputs["a_w_out"][0]),
        "cache_m": inputs["cache_moba_kv"][0].reshape(-1, 2 * NH * HD),
        "b_w_out": np.ascontiguousarray(inputs["b_w_out"][0]),
        "cache_n": inputs["cache_nsa_kv"][0].reshape(-1, 16 * HD),
    }
    wn_ = inputs["b_w_in"][0]
    shared["w_nq_all"] = np.ascontiguousarray(np.stack([np.concatenate(
        [wn_[:, 256 * g:256 * g + 256], wn_[:, 2608 + 256 * g:2608 + 256 * g + 256]], axis=1) for g in range(4)]))
    shared["w_ng_all"] = np.ascontiguousarray(np.stack([np.concatenate(
        [wn_[:, 2560 + 12 * g:2560 + 12 * g + 12], np.zeros((D, 4), np.float32)], axis=1) for g in range(4)]))
    wn = inputs["b_w_in"][0]
    z64 = np.zeros((D, HD), np.float32)
    shared["w_nkv_all"] = np.ascontiguousarray(np.stack([np.concatenate(
        [wn[:, b0 + 64 * g: b0 + 64 * g + 64] for b0 in (1024, 1536, 2048)] + [z64] +
        [wn[:, b0 + 64 * g: b0 + 64 * g + 64] for b0 in (1280, 1792, 2304)] + [z64], axis=1) for g in range(4)]))
    in_maps = [make_inputs(c, inputs, S, past_len, shared) for c in range(NCORES)]
    res = run_bass_kernel_spmd(nc, in_maps, core_ids=list(range(NCORES)))
    return res.results


def kernel(**inputs):
    x_prompt = inputs["x_prompt"]
    B, S, _ = x_prompt.shape
    DB, DS, _ = inputs["x_sample"].shape
    past_len = inputs["page_table"].shape[1] * 128
    r = run(inputs, S, past_len)
    y_prompt = np.zeros((B, S, D), np.float32)
    y_sample = np.zeros((DB, DS, D), np.float32)
    moba_kv_prompt = np.zeros((1, B, S, 2, NH, HD), np.float32)
    moba_kv_sample = np.zeros((1, DB, DS, 2, NH, HD), np.float32)
    nsa_kv_prompt = np.zeros((1, B, S, 4, 4, HD), np.float32)
    nsa_kv_sample = np.zeros((1, DB, DS, 4, 4, HD), np.float32)
    nsa_win_prompt = np.zeros((1, B, min(512, S), 2, 4, HD), np.float32)
    nsa_win_sample = np.zeros((1, DB, 512, 2, 4, HD), np.float32)
    for c in range(NCORES):
        b, hg = c // 4, c % 4
        moba_kv_prompt[0, b, :, :, 4 * hg:4 * hg + 4, :] = r[c]["kvp"]
        moba_kv_sample[0, 4 * c:4 * c + 4] = r[c]["kvs"].reshape(4, 4, 2, NH, HD)
        nsa_kv_prompt[0, b, :, :, hg, :] = r[c]["nkv"]
        nsa_win_prompt[0, b, :, :, hg, :] = r[c]["nwin"]
        SQ = S // 4
        ypc = r[c]["yp"]
        y_prompt[b, hg * SQ:(hg + 1) * SQ] = ypc if ypc.shape[0] == SQ else ypc[hg * SQ:(hg + 1) * SQ]
        y_sample[4 * c:4 * c + 4] = r[c]["ysm"].reshape(4, 4, D)
        nsa_kv_sample[0, 4 * c:4 * c + 4] = r[c]["nkvs"].reshape(4, 4, 4, 4, HD)
        nsa_win_sample[0, 4 * c:4 * c + 4] = r[c]["nwins"]
    return (y_prompt, y_sample, moba_kv_prompt, moba_kv_sample, nsa_kv_prompt, nsa_kv_sample,
            nsa_win_prompt, nsa_win_sample)
```
